# Optimizing a Trainium2 kernel written in Bass

```python
import math
import jax, jax.numpy as jnp
from jax import lax
import numpy as np

D_MODEL = 1024
BATCH = 8
SEQ = 4096
DEPTH = 1

MOBA_HEADS = 8
MOBA_HEAD_DIM = 64
MOBA_WIDTH = MOBA_HEADS * MOBA_HEAD_DIM
MOBA_BLOCK = 256
MOBA_TOPK = 3
MOBA_Q_CHUNK = 16
ROPE_THETA = 10000.0

GLA_HEADS = 4
GLA_DK = D_MODEL // 4
GLA_DV = D_MODEL // 2
GLA_HEAD_K = GLA_DK // GLA_HEADS
GLA_HEAD_V = GLA_DV // GLA_HEADS
GLA_GATE_RANK = 16
GLA_GATE_NORM = 16.0
GLA_CHUNK = 64

N_BRANCH = 2
RMS_EPS = 1e-6
NEG_INF = -1e30

IN_SPLITS = (
    MOBA_WIDTH, MOBA_WIDTH, MOBA_WIDTH, MOBA_WIDTH,
    GLA_DK, GLA_DK, GLA_DV, GLA_DV,
    GLA_GATE_RANK,
    D_MODEL, D_MODEL,
)
N_IN = sum(IN_SPLITS)

kernel_name = "hybrid_moba_gla_gated_merge"


def rms_norm(x, g):
    xf = x.astype(jnp.float32)
    y = xf * lax.rsqrt(jnp.mean(xf * xf, axis=-1, keepdims=True) + RMS_EPS)
    return (y * g.astype(jnp.float32)).astype(x.dtype)


def rotary(t):
    S, dh = t.shape[2], t.shape[3]
    half = dh // 2
    inv_freq = 1.0 / (ROPE_THETA ** (jnp.arange(half, dtype=jnp.float32) / half))
    ang = jnp.arange(S, dtype=jnp.float32)[:, None] * inv_freq[None, :]
    cos, sin = jnp.cos(ang), jnp.sin(ang)
    tf = t.astype(jnp.float32)
    t1, t2 = tf[..., :half], tf[..., half:]
    return jnp.concatenate([t1 * cos - t2 * sin, t1 * sin + t2 * cos], axis=-1).astype(t.dtype)


def moba_attention(q, k, v):
    B, H, S, Dh = q.shape
    S_pad = -(-S // MOBA_BLOCK) * MOBA_BLOCK
    pad = ((0, 0), (0, 0), (0, S_pad - S), (0, 0))
    q, k, v = jnp.pad(q, pad), jnp.pad(k, pad), jnp.pad(v, pad)
    nb = S_pad // MOBA_BLOCK
    k_eff = min(MOBA_TOPK, nb)
    k_blk = k.reshape(B, H, nb, MOBA_BLOCK, Dh)
    v_blk = v.reshape(B, H, nb, MOBA_BLOCK, Dh)
    k_mean = jnp.mean(k_blk.astype(jnp.float32), axis=3)
    n_q = S_pad // MOBA_Q_CHUNK
    q_ch = q.reshape(B, H, n_q, MOBA_Q_CHUNK, Dh).transpose(2, 0, 1, 3, 4)
    scale = Dh ** -0.5
    bi = jnp.arange(B)[:, None, None, None]
    hi = jnp.arange(H)[None, :, None, None]
    blk_ids = jnp.arange(nb)

    def chunk_fn(args):
        qc, ci = args
        pos_q = ci * MOBA_Q_CHUNK + jnp.arange(MOBA_Q_CHUNK)
        own_blk = (ci * MOBA_Q_CHUNK) // MOBA_BLOCK
        gate = jnp.einsum('bhqd,bhnd->bhqn', qc.astype(jnp.float32), k_mean)
        gate = jnp.where(blk_ids < own_blk, gate, NEG_INF)
        _, top_idx = lax.top_k(gate, k_eff)
        own_idx = jnp.broadcast_to(own_blk, top_idx.shape[:-1] + (1,)).astype(top_idx.dtype)
        sel = jnp.concatenate([top_idx, own_idx], axis=-1)
        k_sel = k_blk[bi, hi, sel]
        v_sel = v_blk[bi, hi, sel]
        past_mask = jnp.broadcast_to((jnp.arange(k_eff) < own_blk)[None, :, None],
                                     (MOBA_Q_CHUNK, k_eff, MOBA_BLOCK))
        own_pos = own_blk * MOBA_BLOCK + jnp.arange(MOBA_BLOCK)
        own_mask = (own_pos[None, :] <= pos_q[:, None])[:, None, :]
        mask = jnp.concatenate([past_mask, own_mask], axis=1)
        logits = jnp.einsum('bhqd,bhqnld->bhqnl', qc, k_sel).astype(jnp.float32) * scale
        logits = jnp.where(mask, logits, NEG_INF)
        p = jax.nn.softmax(logits.reshape(B, H, MOBA_Q_CHUNK, -1), axis=-1)
        p = p.reshape(logits.shape).astype(v.dtype)
        return jnp.einsum('bhqnl,bhqnld->bhqd', p, v_sel)

    out = lax.map(chunk_fn, (q_ch, jnp.arange(n_q, dtype=jnp.int32)))
    out = out.transpose(1, 2, 0, 3, 4).reshape(B, H, S_pad, Dh)
    return out[:, :, :S]


def gla_chunked(q, k, v, g):
    B, H, S, Dk = q.shape
    Dv = v.shape[-1]
    nc = S // GLA_CHUNK

    def to_chunks(t):
        return t.astype(jnp.float32).reshape(B, H, nc, GLA_CHUNK, t.shape[-1]).transpose(2, 0, 1, 3, 4)

    causal = jnp.tril(jnp.ones((GLA_CHUNK, GLA_CHUNK), dtype=bool))

    def step(state, inp):
        qc, kc, vc, gc = inp
        b = jnp.cumsum(gc, axis=2)
        o_inter = jnp.einsum('bhtd,bhde->bhte', qc * jnp.exp(b), state)
        diff = b[:, :, :, None, :] - b[:, :, None, :, :]
        decay = jnp.exp(jnp.where(causal[:, :, None], diff, -jnp.inf))
        attn = jnp.einsum('bhtd,bhsd,bhtsd->bhts', qc, kc, decay)
        o = o_inter + jnp.einsum('bhts,bhse->bhte', attn, vc)
        b_last = b[:, :, -1:, :]
        state = (jnp.exp(b_last[:, :, 0, :])[..., None] * state
                 + jnp.einsum('bhsd,bhse->bhde', kc * jnp.exp(b_last - b), vc))
        return state, o

    state0 = jnp.zeros((B, H, Dk, Dv), jnp.float32)
    _, o = lax.scan(step, state0, (to_chunks(q), to_chunks(k), to_chunks(v), to_chunks(g)))
    return o.transpose(1, 2, 0, 3, 4).reshape(B, H, S, Dv)


def setup_inputs(seed: int = 0) -> dict:
    key = jax.random.key(seed)
    ks = jax.random.split(key, 11)
    nrm = jax.random.normal
    return {
        "x": nrm(ks[0], (BATCH, SEQ, D_MODEL), jnp.float32),
        "norm_in_g": 1.0 + 0.01 * nrm(ks[1], (DEPTH, D_MODEL), jnp.float32),
        "w_in": nrm(ks[2], (DEPTH, D_MODEL, N_IN), jnp.float32) * D_MODEL ** -0.5,
        "b_merge": 0.01 * nrm(ks[3], (DEPTH, N_BRANCH, D_MODEL), jnp.float32),
        "w_gla_fg2": nrm(ks[4], (DEPTH, GLA_GATE_RANK, GLA_DK), jnp.float32) * GLA_GATE_RANK ** -0.5,
        "b_gla_fg": 0.1 * nrm(ks[5], (DEPTH, GLA_DK), jnp.float32),
        "gla_norm_g": 1.0 + 0.01 * nrm(ks[6], (DEPTH, GLA_HEAD_V), jnp.float32),
        "w_proj_a": nrm(ks[7], (DEPTH, MOBA_WIDTH, D_MODEL), jnp.float32) * MOBA_WIDTH ** -0.5,
        "w_proj_b": nrm(ks[8], (DEPTH, GLA_DV, D_MODEL), jnp.float32) * GLA_DV ** -0.5,
        "w_out": nrm(ks[9], (DEPTH, D_MODEL, D_MODEL), jnp.float32) * D_MODEL ** -0.5,
        "norm_f_g": 1.0 + 0.01 * nrm(ks[10], (D_MODEL,), jnp.float32),
    }


def reference(x, norm_in_g, w_in, b_merge, w_gla_fg2, b_gla_fg, gla_norm_g,
              w_proj_a, w_proj_b, w_out, norm_f_g):
    B, S, _ = x.shape
    split_pts = [int(p) for p in np.cumsum(IN_SPLITS)[:-1]]

    def heads(t, n):
        return t.reshape(B, S, n, -1).transpose(0, 2, 1, 3)

    for layer in range(DEPTH):
        h = rms_norm(x, norm_in_g[layer])
        proj = jnp.einsum('bsd,de->bse', h, w_in[layer])
        (mq, mk, mv, mgate, gq, gk, gv, ggate, gfg, ga, gb) = jnp.split(proj, split_pts, axis=-1)

        qa = rotary(heads(mq, MOBA_HEADS))
        ka = rotary(heads(mk, MOBA_HEADS))
        va = heads(mv, MOBA_HEADS)
        oa = moba_attention(qa, ka, va)
        oa = oa.transpose(0, 2, 1, 3).reshape(B, S, MOBA_WIDTH) * jax.nn.silu(mgate)
        ya = jnp.einsum('bse,ed->bsd', oa, w_proj_a[layer])

        fg_logit = jnp.einsum('bsr,rk->bsk', gfg, w_gla_fg2[layer]) + b_gla_fg[layer]
        log_alpha = jax.nn.log_sigmoid(fg_logit.astype(jnp.float32)) / GLA_GATE_NORM
        qb = heads(gq, GLA_HEADS) * (GLA_HEAD_K ** -0.5)
        kb = heads(gk, GLA_HEADS)
        vb = heads(gv, GLA_HEADS)
        gdec = heads(log_alpha, GLA_HEADS)
        ob = gla_chunked(qb, kb, vb, gdec)
        ob = rms_norm(ob, gla_norm_g[layer]).astype(x.dtype)
        ob = ob.transpose(0, 2, 1, 3).reshape(B, S, GLA_DV) * jax.nn.silu(ggate)
        yb = jnp.einsum('bse,ed->bsd', ob, w_proj_b[layer])

        merged = (jax.nn.sigmoid(ga + b_merge[layer, 0]) * ya
                  + jax.nn.sigmoid(gb + b_merge[layer, 1]) * yb)
        x = x + jnp.einsum('bsd,de->bse', merged, w_out[layer])

    return rms_norm(x, norm_f_g)
```

```python
import contextlib
import os
import numpy as np
import ml_dtypes
import concourse.bass as bass
import concourse.mybir as mybir
from concourse.bass_utils import run_bass_kernel_spmd

F32 = mybir.dt.float32
BF16 = mybir.dt.bfloat16
AF = mybir.ActivationFunctionType
ALU = mybir.AluOpType
AX = mybir.AxisListType

S = 4096
D = 1024
NIN = 5648
TCH = 512
NCHUNK = S // TCH
EPS = 1e-6
C_MQ, C_MK, C_MV, C_MG, C_GQ, C_GK, C_GV, C_GG, C_FG, C_GA, C_GB = (
    0, 512, 1024, 1536, 2048, 2304, 2560, 3072, 3584, 3600, 4624)
SL_G = {
    0: C_MQ, 1: C_MK, 2: C_MV, 3: C_MG, 4: C_GQ, 5: C_GV, 6: C_GG,
    7: C_GA, 8: C_GA + 512, 9: C_GB, 10: C_GB + 512}
SL_PAB0, SL_PAB1, SL_WO0, SL_WO1 = 11, 12, 15, 16
NSLOT = 17
CHUNK_ORDER = [4, 5, 0, 1, 2, 7, 9, 3, 6, SL_PAB0, 8, 10, SL_PAB1, SL_WO0, SL_WO1]
NWB = 2
STRICT = os.environ.get('STRICT', '1') == '1'


class _Op:
    __slots__ = ("eng", "fn", "deps", "signal", "count", "dsem", "dcount", "idx")


class Sched:
    ENGS = ("pe", "act", "dve", "pool", "sp")

    def __init__(self, nc):
        self.nc = nc
        self.q = {e: [] for e in self.ENGS}
        self.st = {}
        self.ov = {}
        self.dcnt = {}

    def overlap(self, a, b):
        self.ov.setdefault(a, set()).add(b)
        self.ov.setdefault(b, set()).add(a)

    def _s(self, r):
        s = self.st.get(r)
        if s is None:
            s = self.st[r] = [None, []]
        return s

    def add(self, eng, fn, reads=(), writes=(), dsem=None):
        op = _Op()
        op.eng, op.fn, op.signal, op.count, op.dsem, op.dcount = eng, fn, False, 0, dsem, 0
        op.idx = len(self.q[eng])
        isdma = dsem is not None
        deps = []
        for r in reads:
            w = self._s(r)[0]
            if w is not None:
                deps.append(w)
        for r in writes:
            for rr in (r, *self.ov.get(r, ())):
                s = self._s(rr)
                for t in (s[1] if s[1] else ([s[0]] if s[0] is not None else [])):
                    if isdma or t.dsem is not None or t.eng != eng or (STRICT and eng != "pe"):
                        deps.append(t)
        op.deps = deps
        if isdma:
            self.dcnt[dsem] = self.dcnt.get(dsem, 0) + 16
            op.dcount = self.dcnt[dsem]
        self.q[eng].append(op)
        for r in reads:
            self._s(r)[1].append(op)
        for r in writes:
            s = self._s(r)
            s[0] = op
            s[1] = []
        return op

    def emit(self, final_sems=()):
        nc = self.nc
        for e in self.ENGS:
            for op in self.q[e]:
                for d in op.deps:
                    if d.dsem is None:
                        d.signal = True
        for e in self.ENGS:
            c = 0
            for op in self.q[e]:
                if op.dsem is None and op.signal:
                    c += 1
                    op.count = c
        with contextlib.ExitStack() as es:
            esem = {e: es.enter_context(nc.semaphore("s_" + e)) for e in self.ENGS}
            dsem = {k: es.enter_context(nc.semaphore("d_" + k)) for k in self.dcnt}
            block = es.enter_context(nc.Block())

            def run(e, eng):
                waited = {}
                for op in self.q[e]:
                    need = {}
                    for d in op.deps:
                        if d.dsem is not None:
                            k, v = ("d", d.dsem), d.dcount
                        else:
                            k, v = ("e", d.eng), d.count
                        if v > waited.get(k, 0) and v > need.get(k, 0):
                            need[k] = v
                    for k, v in need.items():
                        eng.wait_ge(dsem[k[1]] if k[0] == "d" else esem[k[1]], v)
                        waited[k] = v
                    ins = op.fn(eng)
                    if op.dsem is not None:
                        ins.then_inc(dsem[op.dsem], 16)
                    elif op.signal:
                        ins.then_inc(esem[e], 1)
                if e == "sp":
                    for k in final_sems:
                        eng.wait_ge(dsem[k], self.dcnt[k])

            block.tensor(lambda eng: run("pe", eng))
            block.scalar(lambda eng: run("act", eng))
            block.vector(lambda eng: run("dve", eng))
            block.gpsimd(lambda eng: run("pool", eng))
            block.sync(lambda eng: run("sp", eng))


def _consts():
    bf = ml_dtypes.bfloat16
    c = {}
    c["ident"] = np.eye(128, dtype=np.float32).astype(bf)
    k = np.arange(128)[:, None, None] + 128 * np.arange(4)[None, :, None]
    q = np.arange(512)[None, None, :]
    same = (k // 256) == (q // 256)
    later = (k // 256) > (q // 256)
    cb = np.where((same & (k > q)) | later, -30000.0, 0.0).astype(np.float32)
    c["cb"] = np.ascontiguousarray(cb).astype(bf)
    half = 32
    inv = (1.0 / (10000.0 ** (np.arange(half, dtype=np.float32) / np.float32(half)))).astype(np.float32)
    ang = (np.arange(S, dtype=np.float32)[:, None] * inv[None, :]).astype(np.float32)
    cos = np.cos(ang).astype(np.float32)
    sin = np.sin(ang).astype(np.float32)
    cos2 = np.concatenate([cos, cos], axis=1)
    sin2 = np.concatenate([sin, sin], axis=1)
    c["cos2"] = np.ascontiguousarray(cos2.reshape(32, 128, 64).transpose(1, 0, 2))
    c["sin2"] = np.ascontiguousarray(sin2.reshape(32, 128, 64).transpose(1, 0, 2))
    s_ = np.arange(128)[:, None]
    t_ = np.arange(128)[None, :]
    c["triinc"] = np.where(s_ <= t_, -1.0 / 16.0, 0.0).astype(np.float32)
    c["trirev"] = np.where(s_ > t_, -1.0 / 16.0, 0.0).astype(np.float32)
    c["mask01"] = np.where(s_ <= t_, 1.0, 0.0).astype(np.float32)
    return c


def build(n_chunks=NCHUNK, debug=False, stage=99):
    nc = bass.Bass("TRN2", target_bir_lowering=False)
    dt = nc.dram_tensor
    x = dt("x", [S, D], F32, kind="ExternalInput").ap()
    w_in = dt("w_in", [D, NIN], F32, kind="ExternalInput").ap()
    w_pa = dt("w_pa", [512, D], F32, kind="ExternalInput").ap()
    w_pb = dt("w_pb", [512, D], F32, kind="ExternalInput").ap()
    w_out = dt("w_out", [D, D], F32, kind="ExternalInput").ap()
    g_in = dt("g_in", [128, 8], F32, kind="ExternalInput").ap()
    g_f = dt("g_f", [1, D], F32, kind="ExternalInput").ap()
    g_gn = dt("g_gn", [1, 128], F32, kind="ExternalInput").ap()
    b_mg = dt("b_mg", [1, 2048], F32, kind="ExternalInput").ap()
    w_fg2 = dt("w_fg2", [16, 256], F32, kind="ExternalInput").ap()
    b_fg = dt("b_fg", [1, 256], F32, kind="ExternalInput").ap()
    d_ident = dt("c_ident", [128, 128], BF16, kind="ExternalInput").ap()
    d_cb = dt("c_cb", [128, 4, 512], BF16, kind="ExternalInput").ap()
    d_cos = dt("c_cos2", [128, 32, 64], F32, kind="ExternalInput").ap()
    d_sin = dt("c_sin2", [128, 32, 64], F32, kind="ExternalInput").ap()
    d_tri = dt("c_triinc", [128, 128], F32, kind="ExternalInput").ap()
    d_trr = dt("c_trirev", [128, 128], F32, kind="ExternalInput").ap()
    d_m01 = dt("c_mask01", [128, 128], F32, kind="ExternalInput").ap()
    out = dt("out", [S, D], F32, kind="ExternalOutput").ap()
    wsc = dt("wsc", [NSLOT, 128, 4096], BF16).ap()
    dbg_outs = {}

    es = contextlib.ExitStack()
    with es:
        def sb(name, shape, dtype):
            return es.enter_context(nc.sbuf_tensor(name, shape, dtype))

        def ps(name, shape, dtype=F32):
            return es.enter_context(nc.psum_tensor(name, shape, dtype))

        class Arena:
            def __init__(self, name, nelem):
                self.t = sb(name, [128, nelem], F32)
                self.off = 0
                self.n = nelem

            def reset(self):
                self.off = 0

            def f32(self, n, parts=128):
                v = self.t[0:parts, self.off:self.off + n]
                self.off += n
                assert self.off <= self.n, (self.off, self.n)
                return v

            def bf16(self, n, parts=128):
                assert n % 2 == 0
                v = self.t[0:parts, self.off:self.off + n // 2].bitcast(BF16)
                self.off += n // 2
                assert self.off <= self.n, (self.off, self.n)
                return v

        kT = sb("kT", [128, 4, S], BF16)
        vaug_t = sb("vaug", [128, 32 * 8 * 65], BF16)
        vaug = vaug_t[:, :].rearrange("p (t h d) -> p t h d", t=32, h=8)
        wb = [sb(f"wb{i}", [128, 4096], BF16) for i in range(NWB)]
        ident = sb("ident", [128, 128], BF16)
        cb = sb("cb", [128, 4, 512], BF16)
        cosc = sb("cosc", [128, 4, 64], F32)
        sinc = sb("sinc", [128, 4, 64], F32)
        gfbc = sb("gfbc", [128, D], F32)
        gngbc = sb("gngbc", [128, 128], F32)
        triinc = sb("triinc", [128, 128], F32)
        trirev = sb("trirev", [128, 128], F32)
        mask01 = sb("mask01", [128, 128], F32)
        c16 = sb("c16", [128, 1], F32)
        neghalf = sb("neghalf", [128, 1], F32)
        w2aug = sb("w2aug", [32, 256], F32)
        wfg = sb("wfg", [128, 8, 16], BF16)
        wfg32 = sb("wfg32", [128, 8, 16], F32)
        gin = sb("gin", [128, 8], F32)
        bm2 = sb("bm2", [33, 2048], BF16)
        ones2 = sb("ones2", [33, 128], BF16)
        S32 = sb("S32", [128, 2, 256], F32)
        Sbf = sb("Sbf", [128, 2, 256], BF16)
        kmT = sb("kmT", [128, 4, 16], BF16)
        kms = sb("kms", [128, 4, 2], F32)
        ybuf = sb("ybuf", [128, 4, D], F32)
        hT = sb("hT", [128, 8, TCH], BF16)
        junk = sb("junk", [128, D], BF16)
        ssx = sb("ssx", [128, 4], F32)
        rsx = sb("rsx", [128, 4], F32)
        dec = sb("dec", [128, 2], F32)
        ssg = sb("ssg", [128, 4], F32)
        rsg = sb("rsg", [128, 4], F32)
        ssf = sb("ssf", [128, 4], F32)
        rsf = sb("rsf", [128, 4], F32)
        rden = sb("rden", [128, 4, 1], F32)
        oa = sb("oa", [128, 4, 512], F32)
        ob = sb("ob", [128, 4, 512], F32)
        ar1 = sb("ar1", [128, 5120], F32)
        gqk = ar1[:, 0:2048].rearrange("p (j n) -> p j n", j=4)
        gv = ar1[:, 2048:3072].bitcast(BF16).rearrange("p (j n) -> p j n", j=4)
        qT = ar1[:, 3072:5120].bitcast(BF16).rearrange("p (h n) -> p h n", h=8)
        qT4 = ar1[:, 3072:5120].bitcast(BF16).rearrange("p (pr two n) -> p pr two n", pr=4, two=2)
        sgA = ar1[:, 0:2048].rearrange("p (j n) -> p j n", j=4)
        ar2 = Arena("ar2", 4096)
        hx = [ar2.bf16(D) for _ in range(2)]
        rot1 = [ar2.f32(512) for _ in range(2)]
        rot2 = [ar2.f32(512) for _ in range(2)]
        rotb = [ar2.bf16(512) for _ in range(2)]
        ar2.reset()
        mbf = ar2.bf16(4 * D).rearrange("p (j n) -> p j n", j=4)
        mT = ar2.bf16(8 * TCH).rearrange("p (k n) -> p k n", k=8)
        ar2_A = ["hx0", "hx1", "rot1_0", "rot1_1", "rot2_0", "rot2_1", "rotb0a", "rotb0b", "rotb1a", "rotb1b"]
        ar2_C = ["mbf0", "mbf1", "mbf2", "mbf3", "mT"]
        MBF = ["mbf0", "mbf1", "mbf2", "mbf3"]
        ar3 = Arena("ar3", 6656)
        gfgT = ar3.f32(TCH, parts=32)
        eneg = ar3.f32(256)
        lsp = ar3.f32(256)
        Eb = ar3.f32(512)
        enb = ar3.f32(256)
        qtl = ar3.bf16(256)
        ktl = ar3.bf16(256)
        khat = ar3.bf16(256)
        gqz_flat = ar3.bf16(512)
        gqz = gqz_flat.rearrange("p (h n) -> p h n", h=4)
        gqz4 = gqz_flat.rearrange("p (pr two n) -> p pr two n", pr=2, two=2)
        gkT = ar3.bf16(256).rearrange("p (k n) -> p k n", k=2)
        attn_sb = ar3.bf16(512).rearrange("p (k n) -> p k n", k=4)
        otmp = ar3.f32(512).rearrange("p (k n) -> p k n", k=4)
        pt = [ar3.bf16(1024) for _ in range(3)]
        gate_sb = ar3.f32(128).rearrange("p (h n) -> p h n", h=8)
        max8 = ar3.f32(64).rearrange("p (h n) -> p h n", h=8)
        sel = ar3.f32(512).rearrange("p (q h n) -> p q h n", q=4, h=8)
        acc = [ar3.f32(260).rearrange("p (q d) -> p q d", q=4) for _ in range(2)]
        pvt = [ar3.f32(260).rearrange("p (q d) -> p q d", q=4) for _ in range(2)]
        M8 = [f"max8_{h_}" for h_ in range(8)]
        GT3 = ["gTe", "gTo", "gTk"]
        OTG = [f"otmpg{h_}" for h_ in range(4)]
        ar3_B = ["gfgT", "eneg", "lsp", "Eb", "enb", "qtl", "ktl", "khat", "attn_sb", "otmp",
                 "pt0", "pt1", "pt2", "gate_sb", "sel", "acc0", "acc1", "pvt0", "pvt1"] + M8 + GT3 + OTG
        ar3.reset()
        sil = [ar3.f32(512) for _ in range(2)]
        ogb = [ar3.bf16(512) for _ in range(2)]
        mtmp = [ar3.f32(512) for _ in range(2)]
        oagT = ar3.bf16(4 * TCH).rearrange("p (k n) -> p k n", k=4)
        obgT = ar3.bf16(4 * TCH).rearrange("p (k n) -> p k n", k=4)
        sgB = ar3.f32(2048).rearrange("p (j n) -> p j n", j=4)
        SGA = [f"sgA{j_}" for j_ in range(4)]
        SGB = [f"sgB{j_}" for j_ in range(4)]
        ar3_C = ["sil0", "sil1", "ogb0", "ogb1", "mtmp0", "mtmp1", "oagT", "obgT"] + SGB
        psBig = ps("psBig", [128, 2048])
        psA = [psBig[:, 0:512], psBig[:, 512:1024]]
        psS = [psBig[:, 1024:1536], psBig[:, 1536:2048]]
        psT = [ps(f"psT{i}", [128, 1024], BF16) for i in range(2)]
        psV = [ps(f"psV{i}", [128, 512]) for i in range(2)]

        sc = Sched(nc)
        A = sc.add
        for n_ in SGA:
            sc.overlap("gqk", n_)
        for a_ in ar2_A:
            for b_ in ar2_C:
                sc.overlap(a_, b_)
        for a_ in ar3_B:
            for b_ in ar3_C:
                sc.overlap(a_, b_)

        ucnt = {"n": 0}

        def dma(outap, inap, reads, writes, sem, q="sp"):
            if sem in ("c0", "c1"):
                ucnt["n"] += 1
                sem = f"c{ucnt['n']}"
                if list(writes) != ["gin"]:
                    q = "act"
            return A(q, lambda e: e.dma_start(out=outap, in_=inap), reads, writes, dsem=sem)

        def mm(outap, lhsT, rhs, start, stop, reads, writes, skip=False):
            return A("pe", lambda e: e.matmul(outap, lhsT, rhs, start=start, stop=stop, skip_group_check=skip), reads, writes)

        def tr(outap, inap, reads, writes):
            return A("pe", lambda e: e.transpose(outap, inap, ident[:]), list(reads) + ["ident"], writes)

        dma(gin[:], g_in, [], ["gin"], "c0")
        bmf0 = oa[0:1, :, :]
        bmf32 = oa[32:33, :, :]
        sc.overlap("oa", "bmf0")
        sc.overlap("oa", "bmf32")
        dma(bmf0, b_mg.rearrange("o (j n) -> o j n", j=4), [], ["bmf0"], "c0")
        dma(bmf32, b_mg.rearrange("o (j n) -> o j n", j=4), [], ["bmf32"], "c0")
        A("dve", lambda e: e.memset(bm2[:], 0.0), [], ["bm2"])
        dma(ident[:], d_ident, [], ["ident"], "c0")
        dma(cb[:], d_cb, [], ["cb"], "c0")
        dma(triinc[:], d_tri, [], ["triinc"], "c0")
        dma(trirev[:], d_trr, [], ["trirev"], "c0")
        dma(mask01[:], d_m01, [], ["mask01"], "c0")
        dma(gfbc[:], g_f.partition_broadcast(128)[:, 0, :], [], ["gfbc"], "c0")
        dma(gngbc[:], g_gn.partition_broadcast(128)[:, 0, :], [], ["gngbc"], "c0")
        yb4 = ["ybuf0", "ybuf1", "ybuf2", "ybuf3"]
        A("dve", lambda e: e.memset(w2aug[:], 0.0), [], ["w2aug_a", "w2aug_b"])
        dma(w2aug[0:16, :], w_fg2, [], ["w2aug_a"], "c1")
        dma(w2aug[16:17, :], b_fg, [], ["w2aug_b"], "c1")
        dma(wfg32[:], w_in.rearrange("(kc p) n -> p kc n", p=128)[:, :, C_FG:C_FG + 16], [], ["wfg32"], "c1")
        A("dve", lambda e: e.memset(c16[:], -1.0 / 16.0), [], ["c16"])
        A("dve", lambda e: e.memset(neghalf[:], -0.5), [], ["neghalf"])
        A("dve", lambda e: e.memset(ones2[:], 1.0), [], ["ones2"])
        A("dve", lambda e: e.memset(gfgT[:], 1.0), [], ["gfgT"])
        A("dve", lambda e: e.memset(S32[:], 0.0), [], ["S32_0", "S32_1"])
        A("dve", lambda e: e.memset(Sbf[:], 0.0), [], ["Sbf"])
        A("dve", lambda e: e.memset(kmT[:], 0.0), [], ["kmT"])
        A("dve", lambda e: e.memset(vaug[:, 0:4, :, 64:65], 1.0), [], ["vones"])
        A("dve", lambda e: e.tensor_tensor(wfg[:], wfg32[:], gin[:].unsqueeze(2).broadcast_to([128, 8, 16]), ALU.mult),
          ["wfg32", "gin"], ["wfg"])
        def bm_split():
            bm4 = lambda r: bm2[r:r + 1, :].rearrange("o (j n) -> o j n", j=4)
            t32 = ar2.t[32:33, 0:2048].rearrange("o (j n) -> o j n", j=4)
            A("dve", lambda e: e.tensor_copy(bm4(0), bmf0), ["bmf0"], ["bm2"])
            A("dve", lambda e: e.tensor_copy(bm4(32), bmf32), ["bmf32"], ["bm2"])
            A("dve", lambda e: e.tensor_copy(t32, bm4(32)), ["bm2"], ar2_A)
            A("dve", lambda e: e.tensor_sub(t32, bmf32, t32), ["bmf32"] + ar2_A, ar2_A)
            A("dve", lambda e: e.tensor_copy(bm4(32), t32), ar2_A, ["bm2"])

        if stage == 0:
            srcs_skip = True
        vflat = vaug_t[:, 4 * 520:32 * 520]
        stgA = [vflat[:, 4096 * k_:4096 * (k_ + 1)].bitcast(F32) for k_ in range(3)]
        STG = ["stgA0", "stgA1", "stgA2"]
        for gt_ in range(4, 32):
            for r_ in STG:
                sc.overlap(f"v{gt_}", r_)
        w_in_v = w_in.rearrange("(kc p) n -> p kc n", p=128)
        w_pa_v = w_pa.rearrange("(kc p) n -> p kc n", p=128)
        w_pb_v = w_pb.rearrange("(kc p) n -> p kc n", p=128)
        w_out_v = w_out.rearrange("(kc p) n -> p kc n", p=128)
        src_of = {}
        for sl in range(11):
            v_ = w_in_v[:, :, SL_G[sl]:SL_G[sl] + 512]
            src_of[sl] = ([v_[:, 0:4, :], v_[:, 4:8, :]], True)
        src_of[SL_PAB0] = ([w_pa_v[:, :, 0:512], w_pb_v[:, :, 0:512]], False)
        src_of[SL_PAB1] = ([w_pa_v[:, :, 512:1024], w_pb_v[:, :, 512:1024]], False)
        src_of[SL_WO0] = ([w_out_v[:, 0:4, 0:512], w_out_v[:, 4:8, 0:512]], False)
        src_of[SL_WO1] = ([w_out_v[:, 0:4, 512:1024], w_out_v[:, 4:8, 512:1024]], False)
        units = []
        for i_, sl_ in enumerate(CHUNK_ORDER if stage != 0 else []):
            for hf_ in range(2):
                units.append((i_, sl_, hf_))
        ustate = {"loaded": 0, "cast": 0}

        def unit_load(u):
            i, sl, hf = units[u]
            b = u % 3
            a32 = stgA[b].rearrange("p (k n) -> p k n", k=4)
            dma(a32, src_of[sl][0][hf], [], [f"stgA{b}"], f"ws{b}")

        def unit_cast(u):
            i, sl, hf = units[u]
            b = u % 3
            bw = i % NWB
            a32 = stgA[b].rearrange("p (k n) -> p k n", k=4)
            dst = wb[bw][:, hf * 2048:(hf + 1) * 2048].rearrange("p (k n) -> p k n", k=4)
            if hf == 0:
                if src_of[sl][1]:
                    A("dve", lambda e: e.tensor_tensor(
                        dst, a32, gin[:, 0:4].unsqueeze(2).broadcast_to([128, 4, 512]), ALU.mult),
                      [f"stgA{b}", "gin"], [f"wb{bw}"])
                else:
                    A("dve", lambda e: e.tensor_copy(dst, a32), [f"stgA{b}"], [f"wb{bw}"])
            else:
                if src_of[sl][1]:
                    for kc in range(4):
                        A("act", lambda e, kc=kc: e.activation(dst[:, kc, :], a32[:, kc, :], AF.Copy, scale=gin[:, 4 + kc:5 + kc]),
                          [f"stgA{b}", "gin"], [f"wb{bw}"])
                else:
                    A("act", lambda e: e.copy(dst, a32), [f"stgA{b}"], [f"wb{bw}"])
            if hf == 1:
                dma(wsc[sl, :, :], wb[bw][:, :], [f"wb{bw}"], [f"wsc{sl}"], f"wt{bw}", q="act")

        def cast_tile_into(i, sl):
            while ustate["cast"] < len(units) and units[ustate["cast"]][0] <= i:
                u = ustate["cast"]
                while ustate["loaded"] <= u:
                    unit_load(ustate["loaded"])
                    ustate["loaded"] += 1
                unit_cast(u)
                ustate["cast"] += 1
                while ustate["loaded"] < min(len(units), ustate["cast"] + 3):
                    unit_load(ustate["loaded"])
                    ustate["loaded"] += 1

        wseq = [s_ for _ in range(n_chunks) for s_ in CHUNK_ORDER]
        wstate = {"next": 0}

        def wload(i):
            sl = wseq[i]
            n = 4096
            if i < len(CHUNK_ORDER) and stage != 0:
                cast_tile_into(i, sl)
            else:
                dma(wb[i % NWB][:, 0:n], wsc[sl, :, 0:n], [f"wsc{sl}"], [f"wb{i % NWB}"], f"wb{i % NWB}")

        def getw(i):
            while wstate["next"] < min(len(wseq), i + NWB):
                wload(wstate["next"])
                wstate["next"] += 1
            return wb[i % NWB], f"wb{i % NWB}"

        witer = {"i": 0}

        def nextw(expect):
            i = witer["i"]
            assert wseq[i] == expect, (i, wseq[i], expect)
            witer["i"] += 1
            return getw(i)

        def inproj(wt, wres, j, pbank, pres, ncols=512):
            wv = wt[:, :].rearrange("p (k n) -> p k n", k=8)
            for kc in range(8):
                mm(pbank[:, 0:ncols], hT[:, kc, j * 128:(j + 1) * 128], wv[:, kc, 0:ncols],
                   kc == 0, kc == 7, [f"hT{j}", wres], [pres])

        acount = {"n": 0}

        A3 = [(psA[0], "psA0"), (psA[1], "psA1"), (psV[0], "psV0")]

        def nextA():
            i = acount["n"] % 3
            acount["n"] += 1
            return A3[i]

        class _Stop(Exception):
            pass

        def dump(name, ap, reads):
            d = nc.dram_tensor("dbg_" + name, list(ap.shape), ap.dtype, kind="ExternalOutput").ap()
            dma(d, ap, reads, [], "dbg_" + name)
            dbg_sems.append("dbg_" + name)

        dbg_sems = []

        curc = {"c": 0}

        def chk(k, dumps):
            if stage == k and (k < 2 or curc["c"] == int(os.environ.get("STOPC", "0"))):
                for name, ap, reads in dumps:
                    dump(name, ap, reads)
                raise _Stop()

        def pipeline(tasks, depth=2):
            q_ = []
            for main, post in tasks:
                main()
                q_.append(post)
                if len(q_) > depth:
                    q_.pop(0)()
            while q_:
                q_.pop(0)()

        tcount = {"n": 0}
        prefetched = {"hT": False}
        HT4 = ["hT0", "hT1", "hT2", "hT3"]

        T3 = [(psT[0], "psT0"), (psT[1], "psT1"), (psV[1].bitcast(BF16), "psV1")]

        def nextT():
            i = tcount["n"] % 3
            tcount["n"] += 1
            return T3[i]

        try:
          chk(0, [("bm2", bm2[:], ["bm2"]), ("wfg", wfg[:], ["wfg"]), ("gfbc", gfbc[:], ["gfbc"]), ("w2aug", w2aug[:], ["w2aug_a", "w2aug_b"])])
          chk(1, [("bm2", bm2[:], ["bm2"]), ("wfg", wfg[:], ["wfg"])])
          for c in range(n_chunks):
              curc["c"] = c
              t0 = c * TCH
              if c == 1:
                  A("pool", lambda e: e.memset(vaug[:, 4:32, :, 64:65], 1.0), [], ["vones"] + STG)
              if not prefetched["hT"]:
                  dma(cosc[:], d_cos[:, 4 * c:4 * c + 4, :], [], ["cosc"], "cos")
                  dma(sinc[:], d_sin[:, 4 * c:4 * c + 4, :], [], ["sinc"], "sin")
              oa2 = oa[:].rearrange("p j n -> p (j n)")
              ob2 = ob[:].rearrange("p j n -> p (j n)")
              xs_pref = [(oa2[:, 0:1024], "oa"), (oa2[:, 1024:2048], "oa"), (ob2[:, 0:1024], "ob"), (ob2[:, 1024:2048], "ob")]
              sqd = sil[0].bitcast(BF16)

              def a1_stats(cc, j, xsrc, xres, dummy, dres, load):
                  gt = 4 * cc + j
                  A("act", lambda e: e.activation(dummy, xsrc, AF.Square, accum_out=ssx[:, j:j + 1]),
                    [xres], [dres, f"ssx{j}"])
                  A("dve", lambda e: e.tensor_scalar(rsx[:, j:j + 1], ssx[:, j:j + 1], 1.0 / D, EPS, ALU.mult, ALU.add),
                    [f"ssx{j}"], [f"rsx{j}"])
                  A("pool", lambda e: e.tensor_tensor(rsx[:, j:j + 1], rsx[:, j:j + 1], neghalf[:], ALU.pow),
                    [f"rsx{j}", "neghalf"], [f"rsx{j}"])

              def a1_hx_act(j, xsrc, xres, hb, hres):
                  A("act", lambda e: e.activation(hb[:], xsrc, AF.Copy, scale=rsx[:, j:j + 1]), [xres, f"rsx{j}"], [hres])

              def a1_task(j, xsrc, xres, hb, hres, do_hx=True, post_extra=None):
                  st_ = {}
                  def main():
                      pT, tres = st_["t"] = nextT()
                      if do_hx:
                          A("dve", lambda e: e.tensor_scalar(hb[:], xsrc, rsx[:, j:j + 1], None, ALU.mult),
                            [xres, f"rsx{j}"], [hres])
                      for kc in range(8):
                          tr(pT[:, kc * 128:(kc + 1) * 128], hb[:, kc * 128:(kc + 1) * 128], [hres], [tres])
                  def post():
                      pT, tres = st_["t"]
                      A("act", lambda e: e.copy(hT[:, :, j * 128:(j + 1) * 128], pT[:, :].rearrange("p (k n) -> p k n", k=8)),
                        [tres], [f"hT{j}"])
                      if post_extra is not None:
                          post_extra()
                  return (main, post)

              def load_x_resid():
                  for j in range(4):
                      gt = 4 * c + j
                      dma(ybuf[:, j, :], x[gt * 128:(gt + 1) * 128, :], [], [f"ybuf{j}"], f"x{j}")

              x_early = not prefetched["hT"]
              if x_early:
                  load_x_resid()
              tasks = []
              if not prefetched["hT"]:
                  for j in range(4):
                      a1_stats(c, j, ybuf[:, j, :], f"ybuf{j}", junk[:], "junk", False)
                  tasks = [a1_task(j, ybuf[:, j, :], f"ybuf{j}", hx[j % 2], f"hx{j % 2}") for j in range(4)]
              if c == 0:
                  A("pool", lambda e: e.memset(qT4[0:64, :, 1, :], 0.0), [], ["qT"])
                  A("pool", lambda e: e.memset(qT4[64:128, :, 0, :], 0.0), [], ["qT"])
              a1_tasks = tasks
              if stage == 2:
                  pipeline(tasks)
              chk(2, [("hT", hT[:], HT4), ("rsx", rsx[:], [f"rsx{j}" for j in range(4)])])

              def gla_tile(j, out_list, G0, G1, T0, n0, n1):
                  S_ = out_list.append

                  def s0():
                      mm(G0[:, 0:256], gfgT[:, j * 128:(j + 1) * 128], w2aug[:, :], True, True, ["gfgT", "w2aug_a", "w2aug_b"], [n0])
                  S_(s0)

                  def s1():
                      A("act", lambda e: e.activation(eneg[:], G0[:, 0:256], AF.Exp, scale=-1.0), [n0], ["eneg"])
                      A("act", lambda e: e.activation(lsp[:], eneg[:], AF.Ln, bias=1.0), ["eneg"], ["lsp"])
                  S_(s1)

                  def s2():
                      mm(G1[:, 0:256], triinc[:], lsp[:], True, True, ["triinc", "lsp"], [n1])
                      mm(G1[:, 256:512], trirev[:], lsp[:], True, True, ["trirev", "lsp"], [n1])
                      for pr in range(2):
                          mm(G0[:, 256 + pr:257 + pr], lsp[:, pr * 128:(pr + 1) * 128], c16[:], True, True,
                             ["lsp", "c16"], [n0])
                  S_(s2)

                  def s3():
                      A("act", lambda e: e.activation(Eb[:], G1[:, :], AF.Exp), [n1], ["Eb"])
                      A("act", lambda e: e.activation(enb[:], G1[:, 0:256], AF.Exp, scale=-1.0), [n1], ["enb"])
                      A("act", lambda e: e.activation(dec[:], G0[:, 256:258], AF.Exp), [n0], ["dec"])
                  S_(s3)

                  def s4():
                      A("dve", lambda e: e.scalar_tensor_tensor(qtl[:], gqk[:, j, 0:256], 0.125, Eb[:, 0:256], ALU.mult, ALU.mult),
                        ["gqk", "Eb"], ["qtl"])
                      A("dve", lambda e: e.tensor_mul(ktl[:], gqk[:, j, 256:512], enb[:]), ["gqk", "enb"], ["ktl"])
                      A("pool", lambda e: e.tensor_mul(khat[:], gqk[:, j, 256:512], Eb[:, 256:512]), ["gqk", "Eb"], ["khat"])
                      if j == 0:
                          A("pool", lambda e: e.memset(gqz_flat, 0.0), [], ["gTe", "gTo"])
                  S_(s4)

                  def s5():
                      for pr in range(2):
                          tr(T0[:, pr * 128:(pr + 1) * 128], qtl[:, pr * 128:(pr + 1) * 128], ["qtl"], [n0])
                      for pr in range(2):
                          tr(T0[:, (2 + pr) * 128:(3 + pr) * 128], ktl[:, pr * 128:(pr + 1) * 128], ["ktl"], [n0])
                  S_(s5)

                  def s6():
                      pq = T0[:, 0:256].rearrange("p (k n) -> p k n", k=2)
                      A("dve", lambda e: e.tensor_copy(gqz4[0:64, :, 0, :], pq[0:64]), [n0], ["gTe"])
                      A("dve", lambda e: e.tensor_copy(gqz4[64:128, :, 1, :], pq[64:128]), [n0], ["gTo"])
                      A("dve", lambda e: e.tensor_copy(gkT, T0[:, 256:512].rearrange("p (k n) -> p k n", k=2)), [n0], ["gTk"])
                  S_(s6)

                  def s7():
                      for hg in range(4):
                          mm(G1[:, hg * 128:(hg + 1) * 128], gkT[:, hg // 2, :], gqz[:, hg, :], True, True, GT3, [n1])
                  S_(s7)

                  def s8():
                      A("dve", lambda e: e.tensor_tensor(
                          attn_sb[:], G1[:, :].rearrange("p (h n) -> p h n", h=4),
                          mask01[:].unsqueeze(1).broadcast_to([128, 4, 128]), ALU.mult),
                        [n1, "mask01"], ["attn_sb"])
                  S_(s8)

                  def s9():
                      for hg in range(4):
                          pr = hg // 2
                          mm(G0[:, hg * 128:(hg + 1) * 128], attn_sb[:, hg, :], gv[:, j, hg * 128:(hg + 1) * 128],
                             True, False, ["attn_sb", "gv"], [n0])
                          mm(G0[:, hg * 128:(hg + 1) * 128], gqz[:, hg, :],
                             Sbf[:, pr, (hg % 2) * 128:(hg % 2) * 128 + 128],
                             False, True, ["gTe", "gTo", "Sbf"], [n0])
                      for pr in range(2):
                          mm(G1[:, pr * 256:(pr + 1) * 256], khat[:, pr * 128:(pr + 1) * 128],
                             gv[:, j, pr * 256:(pr + 1) * 256], True, True, ["khat", "gv"], [n1])
                  S_(s9)

                  def s10():
                      for pr in range(2):
                          A("dve", lambda e, pr=pr: e.scalar_tensor_tensor(
                              S32[:, pr, :], S32[:, pr, :], dec[:, pr:pr + 1], G1[:, pr * 256:(pr + 1) * 256],
                              ALU.mult, ALU.add), [f"S32_{pr}", "dec", n1], [f"S32_{pr}"])
                      A("dve", lambda e: e.tensor_copy(Sbf[:], S32[:]), ["S32_0", "S32_1"], ["Sbf"])
                      for hg in range(4):
                          A("act", lambda e, hg=hg: e.activation(otmp[:, hg, :], G0[:, hg * 128:(hg + 1) * 128], AF.Square,
                                                                 accum_out=ssg[:, hg:hg + 1]), [n0], [f"otmpg{hg}", f"ssg{hg}"])
                  S_(s10)

                  def s11():
                      A("dve", lambda e: e.tensor_scalar(rsg[:], ssg[:], 1.0 / 128.0, EPS, ALU.mult, ALU.add), [f"ssg{h_}" for h_ in range(4)], ["rsg"])
                      A("pool", lambda e: e.tensor_tensor(rsg[:], rsg[:], neghalf[:].broadcast_to([128, 4]), ALU.pow),
                        ["rsg", "neghalf"], ["rsg"])
                  S_(s11)

                  def s12():
                      A("dve", lambda e: e.tensor_tensor(
                          otmp[:], G0[:, :].rearrange("p (h n) -> p h n", h=4),
                          rsg[:].unsqueeze(2).broadcast_to([128, 4, 128]), ALU.mult), [n0, "rsg"], ["otmp"] + OTG)
                      A("pool", lambda e: e.tensor_tensor(
                          ob[:, j, :].rearrange("p (h n) -> p h n", h=4), otmp[:],
                          gngbc[:].unsqueeze(1).broadcast_to([128, 4, 128]), ALU.mult), ["otmp", "gngbc"] + OTG, ["ob"])
                  S_(s12)

              tasks = []
              for sl in (0, 1):
                  for j in range(4):
                      st_ = {}
                      def main(sl=sl, j=j, st_=st_):
                          if j == 0:
                              st_["w"] = nextw(sl)
                              wcur[sl] = st_["w"]
                          wt, wres = wcur[sl]
                          st_["p"] = nextA()
                          inproj(wt, wres, j, *st_["p"])
                      def post(sl=sl, j=j, st_=st_):
                          gt = 4 * c + j
                          pb_, pres = st_["p"]
                          k2 = (sl * 4 + j) % 2
                          r1, r2, rb = rot1[k2], rot2[k2], rotb[k2]
                          p3 = pb_[:, :].rearrange("p (h d) -> p h d", h=8)
                          cos_b = cosc[:, j, :].unsqueeze(1).broadcast_to([128, 8, 64])
                          sin_b = sinc[:, j, :].unsqueeze(1).broadcast_to([128, 8, 64])
                          r13 = r1[:, :].rearrange("p (h d) -> p h d", h=8)
                          r23 = r2[:, :].rearrange("p (h d) -> p h d", h=8)
                          rb3 = rb[:, :].rearrange("p (h d) -> p h d", h=8)
                          A("dve", lambda e: e.tensor_tensor(r13, p3, cos_b, ALU.mult), [pres, "cosc"], [f"rot1_{k2}"])
                          A("dve", lambda e: e.tensor_tensor(r23, p3, sin_b, ALU.mult), [pres, "sinc"], [f"rot2_{k2}"])
                          A("pool", lambda e: e.tensor_sub(rb3[:, :, 0:32], r13[:, :, 0:32], r23[:, :, 32:64]),
                            [f"rot1_{k2}", f"rot2_{k2}"], [f"rotb{k2}a"])
                          A("pool", lambda e: e.tensor_add(rb3[:, :, 32:64], r13[:, :, 32:64], r23[:, :, 0:32]),
                            [f"rot1_{k2}", f"rot2_{k2}"], [f"rotb{k2}b"])
                          pT, tres = nextT()
                          for pr in range(4):
                              tr(pT[:, pr * 128:(pr + 1) * 128], rb[:, pr * 128:(pr + 1) * 128], [f"rotb{k2}a", f"rotb{k2}b"], [tres])
                          src = pT[:, 0:512].rearrange("p (k n) -> p k n", k=4)
                          if sl == 0:
                              A("act", lambda e: e.copy(qT4[0:64, :, 0, j * 128:(j + 1) * 128], src[0:64]), [tres], ["qT"])
                              A("act", lambda e: e.copy(qT4[64:128, :, 1, j * 128:(j + 1) * 128], src[64:128]), [tres], ["qT"])
                          else:
                              A("act", lambda e: e.copy(kT[:, :, gt * 128:(gt + 1) * 128], src), [tres], [f"kT{gt}"])
                      tasks.append((main, post))
              wcur = {}
              qk_tasks = tasks
              tasks = []
              for sl, kind in ((4, "gqk"), (5, "gv"), (2, "v")):
                  for j in range(4):
                      st_ = {}
                      def main(sl=sl, j=j, st_=st_):
                          if j == 0:
                              wcur[sl] = nextw(sl)
                          wt, wres = wcur[sl]
                          st_["p"] = nextA()
                          inproj(wt, wres, j, *st_["p"])
                      def post(kind=kind, j=j, st_=st_):
                          gt = 4 * c + j
                          pb_, pres = st_["p"]
                          if kind == "v":
                              A("act", lambda e: e.copy(vaug[:, gt, :, 0:64], pb_[:, :].rearrange("p (h d) -> p h d", h=8)),
                                [pres], [f"v{gt}"])
                          elif kind == "gqk":
                              A("dve", lambda e: e.tensor_copy(gqk[:, j, :], pb_[:, :]), [pres], ["gqk"])
                          else:
                              A("act", lambda e: e.copy(gv[:, j, :], pb_[:, :]), [pres], ["gv"])
                      tasks.append((main, post))
              g_tasks, v_tasks = tasks[0:8], tasks[8:12]
              st_g = {}
              def main_g():
                  st_g["p"] = nextA()
                  pb_, pres = st_g["p"]
                  for kc in range(8):
                      mm(pb_[0:16, :], wfg[:, kc, :], hT[:, kc, :], kc == 0, kc == 7, HT4 + ["wfg"], [pres])
              def post_g():
                  pb_, pres = st_g["p"]
                  A("pool", lambda e: e.memset(gfgT, 1.0), [], ["gfgT"])
                  A("act", lambda e: e.copy(gfgT[0:16, :], pb_[0:16, :]), [pres], ["gfgT"])
              a2 = g_tasks + [(main_g, post_g)] + qk_tasks + v_tasks
              tasks = a2
              if a1_tasks:
                  tasks = [a1_tasks[0], a1_tasks[1], a1_tasks[2], a2[0], a1_tasks[3]] + a2[1:]
              def kmean_compute():
                  A("dve", lambda e, t0=t0: e.tensor_reduce(
                      kms[:], kT[:, :, t0:t0 + TCH].rearrange("p k (b n) -> p k b n", b=2), AX.X, ALU.add),
                    [f"kT{4 * c + j}" for j in range(4)], ["kms"])
                  A("dve", lambda e, c=c: e.tensor_scalar(kmT[:, :, 2 * c:2 * c + 2], kms[:], 1.0 / 256.0, None, ALU.mult),
                    ["kms"], ["kmT"])
                  A("dve", lambda e: e.memset(sel[:], 0.0), [], ["sel"])

              def sel_compute(qi):
                  own = 2 * c + qi // 2
                  if own <= 3:
                      A("dve", lambda e, qi=qi, own=own: e.memset(sel[:, qi, :, 0:own + 1], 1.0), [], ["sel"])
                      return
                  pG, gres = psV[1], "psV1"
                  for h in range(8):
                      mm(pG[:, h * 16:(h + 1) * 16], qT[:, h, qi * 128:(qi + 1) * 128],
                         kmT[:, h // 2, :], True, True, ["qT", "kmT"], [gres])
                  A("dve", lambda e, pG=pG: e.tensor_copy(gate_sb[:], pG[:, 0:128].rearrange("p (h n) -> p h n", h=8)),
                    [gres], ["gate_sb"])
                  if own < 16:
                      A("dve", lambda e, own=own: e.memset(gate_sb[:, :, own:16], -1e30), [], ["gate_sb"])
                  for h in range(8):
                      A("dve", lambda e, h=h: e.max(max8[:, h, :], gate_sb[:, h, :]), ["gate_sb"], [f"max8_{h}"])
                  A("dve", lambda e, qi=qi: e.tensor_tensor(
                      sel[:, qi, :, :], gate_sb[:], max8[:, :, 2:3].broadcast_to([128, 8, 16]), ALU.is_ge),
                    ["gate_sb"] + M8, ["sel"])
                  A("dve", lambda e, qi=qi, own=own: e.memset(sel[:, qi, :, own:own + 1], 1.0), [], ["sel"])


              early_gla = []
              n_early = (2 if c <= 3 else (1 if c <= 5 else 0)) if stage > 90 else 0
              if n_early:
                  for j in range(n_early):
                      gla_tile(j, early_gla, psS[0], psS[1], psS[0].bitcast(BF16), "psS0", "psS1")
                  ne = len(early_gla)
                  base = len(tasks) - 12
                  for t_ in range(12):
                      lo, hi = (t_ * ne) // 12, ((t_ + 1) * ne) // 12
                      m_, p_ = tasks[base + t_]
                      tasks[base + t_] = (m_, (lambda p_=p_, lo=lo, hi=hi: (p_(), [early_gla[q_]() for q_ in range(lo, hi)])))
              hooks = [[kmean_compute], [lambda: sel_compute(0)], [lambda: sel_compute(1)],
                       [lambda: sel_compute(2), lambda: sel_compute(3)]]
              for t_ in range(4):
                  m_, p_ = tasks[len(tasks) - 4 + t_]
                  tasks[len(tasks) - 4 + t_] = (m_, (lambda p_=p_, hk=hooks[t_]: (p_(), [h_() for h_ in hk])))
              pipeline(tasks)
              if c == 0:
                  bm_split()
              chk(3, [("qT", qT, ["qT"]), ("kT", kT[:, :, 0:512], [f"kT{j}" for j in range(4)]), ("v", vaug[:, 0:4], [f"v{j}" for j in range(4)]), ("kmT", kmT[:], ["kmT"])])
              chk(4, [("gqk", gqk, ["gqk"]), ("gv", gv, ["gv"]), ("gfgT", gfgT, ["gfgT"])])

              gla_stages = []
              G0f, G1f = psT[0][:, :].bitcast(F32), psT[1][:, :].bitcast(F32)
              for j in range(n_early, 4):
                  gla_tile(j, gla_stages, G0f, G1f, psT[0], "psT0", "psT1")
              if stage == 5:
                  for s_ in gla_stages:
                      s_()
                  gla_stages = []
              chk(5, [("ob", ob[:], ["ob"]), ("S32", S32[:], ["S32_0", "S32_1"]), ("lsp", lsp, ["lsp"]), ("Eb", Eb, ["Eb"])])

              nblk = 2 * c + 2
              mtasks = [(h, n) for h in range(8) for n in range(nblk)]
              SREG = [(psBig[:, 0:1024], ["psA0", "psA1"]), (psBig[:, 1024:2048], ["psS0", "psS1"])]

              def m_ST(idx):
                  h, n = mtasks[idx]
                  region, rres = SREG[idx % 2]
                  inchunk = n >= 2 * c
                  q0 = 256 if n == 2 * c + 1 else 0
                  for kk in range(2):
                      kt = 2 * n + kk
                      jk = kt - 4 * c
                      mm(region[:, kk * 512 + q0:(kk + 1) * 512], kT[:, h // 2, kt * 128:(kt + 1) * 128], qT[:, h, q0:512],
                         True, not inchunk, [f"kT{kt}", "qT"], [rres[kk]])
                      if inchunk:
                          mm(region[:, kk * 512 + q0:(kk + 1) * 512], ident[:], cb[:, jk, q0:512], False, True,
                             ["ident", "cb"], [rres[kk]])

              def m_exp(idx):
                  region, rres = SREG[idx % 2]
                  pbuf, pres_ = pt[idx % 3], f"pt{idx % 3}"
                  if mtasks[idx][1] == 2 * c + 1:
                      src = region.rearrange("p (k n) -> p k n", k=2)[:, :, 256:512]
                      dst = pbuf.rearrange("p (k n) -> p k n", k=2)[:, :, 256:512]
                      A("act", lambda e: e.activation(dst, src, AF.Exp, scale=0.125), rres, [pres_])
                  else:
                      A("act", lambda e: e.activation(pbuf[:], region, AF.Exp, scale=0.125), rres, [pres_])

              def m_rest(idx):
                  h, n = mtasks[idx]
                  region, rres = SREG[idx % 2]
                  pbuf, pres_ = pt[idx % 3], f"pt{idx % 3}"
                  vb = idx % 2
                  aE, aO = acc[0], acc[1]
                  inchunk = n >= 2 * c
                  qlo = 0 if n <= 2 * c else 2
                  round_first = True
                  for kk in range(2):
                      kt = 2 * n + kk
                      jk = kt - 4 * c
                      for qi in range(qlo, 4):
                          if inchunk and qi < jk:
                              continue
                          last = (kk == 1) or (inchunk and qi == jk)
                          mm(psV[vb][:, qi * 65:(qi + 1) * 65], pbuf[:, kk * 512 + qi * 128:kk * 512 + (qi + 1) * 128],
                             vaug[:, kt, h, :], round_first, last, [pres_, f"v{kt}", "vones"], [f"psV{vb}"], skip=True)
                          round_first = False
                  pv3 = psV[vb][:, 0:260].rearrange("p (q d) -> p q d", q=4)
                  selv = sel[:, qlo:4, h, n:n + 1].broadcast_to([128, 4 - qlo, 65])
                  if n == 0:
                      A("dve", lambda e: e.tensor_tensor(aE[:], pv3, selv, ALU.mult), [f"psV{vb}", "sel"], ["acc0"])
                  elif n == 1:
                      A("dve", lambda e: e.tensor_tensor(aO[:, qlo:4, :], pv3[:, qlo:4, :], selv, ALU.mult),
                        [f"psV{vb}", "sel"], ["acc1"])
                      if qlo > 0:
                          A("dve", lambda e: e.memset(aO[:, 0:qlo, :], 0.0), [], ["acc1"])
                  else:
                      tb = pvt[vb]
                      A("dve", lambda e: e.tensor_tensor(tb[:, qlo:4, :], pv3[:, qlo:4, :], selv, ALU.mult),
                        [f"psV{vb}", "sel"], [f"pvt{vb}"])
                      if n % 2 == 0:
                          A("pool", lambda e: e.tensor_add(aE[:, qlo:4, :], aE[:, qlo:4, :], tb[:, qlo:4, :]),
                            [f"pvt{vb}", "acc0"], ["acc0"])
                      else:
                          A("dve", lambda e: e.tensor_add(aO[:, qlo:4, :], aO[:, qlo:4, :], tb[:, qlo:4, :]),
                            [f"pvt{vb}", "acc1"], ["acc1"])
                  if n == nblk - 1:
                      A("dve", lambda e: e.tensor_add(aE[:], aE[:], aO[:]), ["acc0", "acc1"], ["acc0"])
                      A("dve", lambda e: e.reciprocal(rden[:], aE[:, :, 64:65]), ["acc0"], ["rden"])
                      A("dve", lambda e: e.tensor_tensor(oa[:, :, h * 64:(h + 1) * 64], aE[:, :, 0:64],
                                                         rden[:].broadcast_to([128, 4, 64]), ALU.mult),
                        ["acc0", "rden"], ["oa"])

              nmt = len(mtasks)
              gq_ = list(gla_stages)
              ngs = len(gq_)

              m_ST(0)
              if nmt > 1:
                  m_ST(1)
              for i in range(nmt):
                  m_exp(i)
                  if i + 2 < nmt:
                      m_ST(i + 2)
                  m_rest(i)
                  while gq_ and (ngs - len(gq_)) < ((i + 1) * ngs + nmt - 1) // nmt:
                      gq_.pop(0)()
              while gq_:
                  gq_.pop(0)()
              chk(6, [("oa", oa[:], ["oa"]), ("sel", sel, ["sel"])])

              if not x_early:
                  load_x_resid()
              tasks = []
              wcur = {}

              def add_gated(slot, src, srcres, dstT, dres):
                  for j in range(4):
                      st_ = {}
                      def main(j=j, st_=st_):
                          if j == 0:
                              wcur[slot] = nextw(slot)
                          wt, wres = wcur[slot]
                          st_["p"] = nextA()
                          inproj(wt, wres, j, *st_["p"])
                      def post(j=j, st_=st_):
                          pb_, pres = st_["p"]
                          sb_, sres = sil[j % 2], f"sil{j % 2}"
                          gb_, gres = ogb[j % 2], f"ogb{j % 2}"
                          A("act", lambda e: e.activation(sb_[:], pb_[:, :], AF.Silu), [pres], [sres])
                          A("dve", lambda e: e.tensor_mul(gb_[:], src[:, j, :], sb_[:]), [sres, srcres], [gres])
                          pT, tres = nextT()
                          for fc in range(4):
                              tr(pT[:, fc * 128:(fc + 1) * 128], gb_[:, fc * 128:(fc + 1) * 128], [gres], [tres])
                          A("act", lambda e: e.copy(dstT[:, :, j * 128:(j + 1) * 128],
                                                    pT[:, 0:512].rearrange("p (k n) -> p k n", k=4)), [tres], [dres])
                      tasks.append((main, post))

              def add_gate_half(slot, dst, dres, bcol):
                  for j in range(4):
                      st_ = {}
                      def main(j=j, st_=st_):
                          if j == 0:
                              wcur[slot] = nextw(slot)
                          wt, wres = wcur[slot]
                          pb_, pres = st_["p"] = nextA()
                          wv = wt[:, :].rearrange("p (k n) -> p k n", k=8)
                          for kc in range(8):
                              mm(pb_[:, :], hT[:, kc, j * 128:(j + 1) * 128], wv[:, kc, :], kc == 0, False, [f"hT{j}", wres], [pres])
                          mm(pb_[:, :], ones2[:, :], bm2[:, bcol:bcol + 512], False, True, ["ones2", "bm2"], [pres])
                      def post(j=j, st_=st_):
                          pb_, pres = st_["p"]
                          A("act", lambda e: e.activation(dst[:, j, :], pb_[:, :], AF.Sigmoid), [pres], [f"{dres}{j}"])
                      tasks.append((main, post))

              def add_proj_half(slot, srcT, sres, is_b, half):
                  for j in range(4):
                      st_ = {}
                      def main(j=j, st_=st_):
                          if j == 0 and not is_b:
                              wcur[slot] = nextw(slot)
                          wt, wres = wcur[slot]
                          wv = (wt[:, 2048:4096] if is_b else wt[:, 0:2048]).rearrange("p (k n) -> p k n", k=4)
                          pb_, pres = st_["p"] = nextA()
                          for kc in range(4):
                              mm(pb_[:, :], srcT[:, kc, j * 128:(j + 1) * 128], wv[:, kc, :], kc == 0, kc == 3, [sres, wres], [pres])
                      def post(j=j, st_=st_):
                          pb_, pres = st_["p"]
                          if not is_b:
                              A("dve", lambda e: e.tensor_mul(sgA[:, j, :], sgA[:, j, :], pb_[:, :]), [f"sgA{j}", pres], [f"sgA{j}"])
                          else:
                              mt, mres = mtmp[j % 2], f"mtmp{j % 2}"
                              A("dve", lambda e: e.tensor_mul(mt[:], sgB[:, j, :], pb_[:, :]), [f"sgB{j}", pres], [mres])
                              A("pool", lambda e: e.tensor_add(mbf[:, j, half * 512:(half + 1) * 512], sgA[:, j, :], mt[:]),
                                [f"sgA{j}", mres], [f"mbf{j}"])
                      tasks.append((main, post))

              do_pref = (c + 1 < n_chunks) and stage > 90
              a1p = []
              for half, (sga_sl, sgb_sl, pa_sl, pb_sl) in enumerate(((7, 9, SL_PAB0, SL_PAB0), (8, 10, SL_PAB1, SL_PAB1))):
                  n9 = len(tasks)
                  add_gate_half(sga_sl, sgA, "sgA", half * 512)
                  if half == 1 and do_pref:
                      for j in range(4):
                          m_, p_ = tasks[n9 + j]
                          tasks[n9 + j] = (m_, (lambda p_=p_, j=j: (p_(), a1_stats(c + 1, j, xs_pref[j][0], xs_pref[j][1], sqd, "sil0", False))))
                  n0 = len(tasks)
                  add_gate_half(sgb_sl, sgB, "sgB", 1024 + half * 512)
                  if half == 0:
                      add_gated(3, oa, "oa", oagT, "oagT")
                      add_gated(6, ob, "ob", obgT, "obgT")
                  if half == 1 and do_pref:
                      hxb = [(junk, "junk"), (sil[1].bitcast(BF16), "sil1")]
                      for j in range(2):
                          m_, p_ = tasks[n0 + j]
                          tasks[n0 + j] = (m_, (lambda p_=p_, j=j: (p_(), a1_hx_act(j, xs_pref[j][0], xs_pref[j][1], *hxb[j % 2]))))
                      for j in range(4):
                          pe_ = None
                          if j + 2 < 4:
                              pe_ = (lambda j=j: a1_hx_act(j + 2, xs_pref[j + 2][0], xs_pref[j + 2][1], *hxb[j % 2]))
                          a1p.append(a1_task(j, xs_pref[j][0], xs_pref[j][1], *hxb[j % 2], do_hx=False, post_extra=pe_))
                  add_proj_half(pa_sl, oagT, "oagT", False, half)
                  if half == 0:
                      if do_pref:
                          def pf_main():
                              dma(cosc[:], d_cos[:, 4 * (c + 1):4 * (c + 1) + 4, :], [], ["cosc"], "cos")
                              dma(sinc[:], d_sin[:, 4 * (c + 1):4 * (c + 1) + 4, :], [], ["sinc"], "sin")
                              for j in range(4):
                                  gt = 4 * (c + 1) + j
                                  dma(xs_pref[j][0], x[gt * 128:(gt + 1) * 128, :], [], [xs_pref[j][1]], f"xp{j}")
                          tasks.insert(len(tasks) - 2, (pf_main, lambda: None))
                      add_proj_half(pb_sl, obgT, "obgT", True, half)
                  else:
                      n1 = len(tasks)
                      add_proj_half(pb_sl, obgT, "obgT", True, half)
                      pbt = tasks[n1:]
                      del tasks[n1:]
                      mtt = []
                      for j in range(4):
                          st_ = {}
                          def main(j=j, st_=st_):
                              pT, tres = st_["t"] = nextT()
                              for kc in range(8):
                                  tr(pT[:, kc * 128:(kc + 1) * 128], mbf[:, j, kc * 128:(kc + 1) * 128], [f"mbf{j}"], [tres])
                          def post(j=j, st_=st_):
                              pT, tres = st_["t"]
                              A("act", lambda e: e.copy(mT[:, :, j * 128:(j + 1) * 128],
                                                        pT[:, :].rearrange("p (k n) -> p k n", k=8)), [tres], ["mT"])
                          mtt.append((main, post))
                      order_ = [pbt[0], pbt[1]] + a1p[0:1] + [pbt[2], mtt[0], pbt[3], mtt[1]] + a1p[1:2] + [mtt[2]] \
                          + a1p[2:3] + [mtt[3]] + a1p[3:4]
                      tasks.extend(order_)
              prefetched["hT"] = do_pref
              for half, sl in enumerate((SL_WO0, SL_WO1)):
                  for j in range(4):
                      st_ = {}
                      def main(j=j, st_=st_, sl=sl):
                          if j == 0:
                              wcur[sl] = nextw(sl)
                          wt, wres = wcur[sl]
                          wv = wt[:, :].rearrange("p (k n) -> p k n", k=8)
                          pb_, pres = st_["p"] = nextA()
                          for kc in range(8):
                              mm(pb_[:, :], mT[:, kc, j * 128:(j + 1) * 128], wv[:, kc, :], kc == 0, kc == 7, ["mT", wres], [pres])
                      def post(j=j, st_=st_, half=half):
                          pb_, pres = st_["p"]
                          A("dve", lambda e: e.tensor_add(
                              ybuf[:, j, half * 512:(half + 1) * 512], ybuf[:, j, half * 512:(half + 1) * 512], pb_[:, :]),
                            [pres, f"ybuf{j}"], [f"ybuf{j}"])
                          if half == 1:
                              final_norm(j)
                      tasks.append((main, post))

              pend_out = []

              def flush_out():
                  while pend_out:
                      jj = pend_out.pop(0)
                      gg = 4 * c + jj
                      dma(out[gg * 128:(gg + 1) * 128, :], ybuf[:, jj, :], [f"ybuf{jj}"], [], f"o{jj}", q="act")

              def final_norm(j):
                  gt = 4 * c + j
                  yj = f"ybuf{j}"
                  flush_out()
                  A("act", lambda e, j=j: e.activation(junk[:], ybuf[:, j, :], AF.Square, accum_out=ssf[:, j:j + 1]),
                    [yj], ["junk", f"ssf{j}"])
                  A("dve", lambda e, j=j: e.tensor_scalar(rsf[:, j:j + 1], ssf[:, j:j + 1], 1.0 / D, EPS, ALU.mult, ALU.add),
                    [f"ssf{j}"], [f"rsf{j}"])
                  A("pool", lambda e, j=j: e.tensor_tensor(rsf[:, j:j + 1], rsf[:, j:j + 1], neghalf[:], ALU.pow),
                    [f"rsf{j}", "neghalf"], [f"rsf{j}"])
                  A("dve", lambda e, j=j: e.scalar_tensor_tensor(ybuf[:, j, :], ybuf[:, j, :], rsf[:, j:j + 1], gfbc[:],
                                                                 ALU.mult, ALU.mult), [yj, f"rsf{j}", "gfbc"], [yj])
                  pend_out.append(j)

              pipeline(tasks)
              flush_out()
        except _Stop:
            pass
        sc.emit(final_sems=[k_ for k_ in [f"o{j}" for j in range(4)] + dbg_sems if k_ in sc.dcnt])
    return nc


_CACHE = {}


def _host_inputs(inputs):
    cst = _consts()
    f = lambda a: np.ascontiguousarray(np.asarray(a, dtype=np.float32))
    common = {
        "w_in": f(inputs["w_in"][0]),
        "w_pa": f(inputs["w_proj_a"][0]),
        "w_pb": f(inputs["w_proj_b"][0]),
        "w_out": f(inputs["w_out"][0]),
        "g_in": np.ascontiguousarray(f(inputs["norm_in_g"][0]).reshape(8, 128).T),
        "g_f": f(inputs["norm_f_g"]).reshape(1, D),
        "g_gn": f(inputs["gla_norm_g"][0]).reshape(1, 128),
        "b_mg": f(inputs["b_merge"][0]).reshape(1, 2048),
        "w_fg2": f(inputs["w_gla_fg2"][0]),
        "b_fg": f(inputs["b_gla_fg"][0]).reshape(1, 256),
        "c_ident": cst["ident"], "c_cb": cst["cb"], "c_cos2": cst["cos2"], "c_sin2": cst["sin2"],
        "c_triinc": cst["triinc"], "c_trirev": cst["trirev"], "c_mask01": cst["mask01"],
    }
    return common


def kernel(x, norm_in_g, w_in, b_merge, w_gla_fg2, b_gla_fg, gla_norm_g,
           w_proj_a, w_proj_b, w_out, norm_f_g):
    inputs = dict(x=x, norm_in_g=norm_in_g, w_in=w_in, b_merge=b_merge, w_gla_fg2=w_gla_fg2,
                  b_gla_fg=b_gla_fg, gla_norm_g=gla_norm_g, w_proj_a=w_proj_a, w_proj_b=w_proj_b,
                  w_out=w_out, norm_f_g=norm_f_g)
    common = _host_inputs(inputs)
    xs = np.asarray(x, dtype=np.float32)
    n = xs.shape[0]
    if "nc" not in _CACHE:
        _CACHE["nc"] = build()
    nc = _CACHE["nc"]
    in_maps = [dict(common, x=np.ascontiguousarray(xs[i])) for i in range(n)]
    res = run_bass_kernel_spmd(nc, in_maps, core_ids=list(range(n)))
    return np.stack([np.asarray(r["out"], dtype=np.float32) for r in res.results], axis=0)
```

```python
import contextlib
import os
import numpy as np
import ml_dtypes
import concourse.bass as bass
import concourse.mybir as mybir
from concourse.bass_utils import run_bass_kernel_spmd

F32 = mybir.dt.float32
BF16 = mybir.dt.bfloat16
AF = mybir.ActivationFunctionType
ALU = mybir.AluOpType
AX = mybir.AxisListType

S = 4096
D = 1024
NIN = 5648
TCH = 512
NCHUNK = S // TCH
EPS = 1e-6
C_MQ, C_MK, C_MV, C_MG, C_GQ, C_GK, C_GV, C_GG, C_FG, C_GA, C_GB = (
    0, 512, 1024, 1536, 2048, 2304, 2560, 3072, 3584, 3600, 4624)
SL_G = {
    0: C_MQ, 1: C_MK, 2: C_MV, 3: C_MG, 4: C_GQ, 5: C_GV, 6: C_GG,
    7: C_GA, 8: C_GA + 512, 9: C_GB, 10: C_GB + 512}
SL_PAB0, SL_PAB1, SL_WO0, SL_WO1 = 11, 12, 15, 16
NSLOT = 17
CHUNK_ORDER = [4, 5, 0, 1, 2, 7, 9, 3, 6, SL_PAB0, 8, 10, SL_PAB1, SL_WO0, SL_WO1]
NWB = 2
STRICT = os.environ.get('STRICT', '0') == '1'


class _Op:
    __slots__ = ("eng", "fn", "deps", "signal", "count", "dsem", "dcount", "idx")


class Sched:
    ENGS = ("pe", "act", "dve", "pool", "sp")

    def __init__(self, nc):
        self.nc = nc
        self.q = {e: [] for e in self.ENGS}
        self.st = {}
        self.ov = {}
        self.dcnt = {}

    def overlap(self, a, b):
        self.ov.setdefault(a, set()).add(b)
        self.ov.setdefault(b, set()).add(a)

    def _s(self, r):
        s = self.st.get(r)
        if s is None:
            s = self.st[r] = [None, []]
        return s

    def add(self, eng, fn, reads=(), writes=(), dsem=None):
        op = _Op()
        op.eng, op.fn, op.signal, op.count, op.dsem, op.dcount = eng, fn, False, 0, dsem, 0
        op.idx = len(self.q[eng])
        isdma = dsem is not None
        deps = []
        for r in reads:
            w = self._s(r)[0]
            if w is not None:
                deps.append(w)
        for r in writes:
            for rr in (r, *self.ov.get(r, ())):
                s = self._s(rr)
                for t in (s[1] if s[1] else ([s[0]] if s[0] is not None else [])):
                    if isdma or t.dsem is not None or t.eng != eng or (STRICT and eng != "pe"):
                        deps.append(t)
        op.deps = deps
        if isdma:
            self.dcnt[dsem] = self.dcnt.get(dsem, 0) + 16
            op.dcount = self.dcnt[dsem]
        self.q[eng].append(op)
        for r in reads:
            self._s(r)[1].append(op)
        for r in writes:
            s = self._s(r)
            s[0] = op
            s[1] = []
        return op

    def emit(self, final_sems=()):
        nc = self.nc
        for e in self.ENGS:
            for op in self.q[e]:
                for d in op.deps:
                    if d.dsem is None:
                        d.signal = True
        for e in self.ENGS:
            c = 0
            for op in self.q[e]:
                if op.dsem is None and op.signal:
                    c += 1
                    op.count = c
        with contextlib.ExitStack() as es:
            esem = {e: es.enter_context(nc.semaphore("s_" + e)) for e in self.ENGS}
            dsem = {k: es.enter_context(nc.semaphore("d_" + k)) for k in self.dcnt}
            block = es.enter_context(nc.Block())

            def run(e, eng):
                waited = {}
                for op in self.q[e]:
                    need = {}
                    for d in op.deps:
                        if d.dsem is not None:
                            k, v = ("d", d.dsem), d.dcount
                        else:
                            k, v = ("e", d.eng), d.count
                        if v > waited.get(k, 0) and v > need.get(k, 0):
                            need[k] = v
                    for k, v in need.items():
                        eng.wait_ge(dsem[k[1]] if k[0] == "d" else esem[k[1]], v)
                        waited[k] = v
                    ins = op.fn(eng)
                    if op.dsem is not None:
                        ins.then_inc(dsem[op.dsem], 16)
                    elif op.signal:
                        ins.then_inc(esem[e], 1)
                if e == "sp":
                    for k in final_sems:
                        eng.wait_ge(dsem[k], self.dcnt[k])

            block.tensor(lambda eng: run("pe", eng))
            block.scalar(lambda eng: run("act", eng))
            block.vector(lambda eng: run("dve", eng))
            block.gpsimd(lambda eng: run("pool", eng))
            block.sync(lambda eng: run("sp", eng))


def _consts():
    bf = ml_dtypes.bfloat16
    c = {}
    c["ident"] = np.eye(128, dtype=np.float32).astype(bf)
    k = np.arange(128)[:, None, None] + 128 * np.arange(4)[None, :, None]
    q = np.arange(512)[None, None, :]
    same = (k // 256) == (q // 256)
    later = (k // 256) > (q // 256)
    cb = np.where((same & (k > q)) | later, -30000.0, 0.0).astype(np.float32)
    c["cb"] = np.ascontiguousarray(cb).astype(bf)
    half = 32
    inv = (1.0 / (10000.0 ** (np.arange(half, dtype=np.float32) / np.float32(half)))).astype(np.float32)
    ang = (np.arange(S, dtype=np.float32)[:, None] * inv[None, :]).astype(np.float32)
    cos = np.cos(ang).astype(np.float32)
    sin = np.sin(ang).astype(np.float32)
    cos2 = np.concatenate([cos, cos], axis=1)
    sin2 = np.concatenate([sin, sin], axis=1)
    c["cos2"] = np.ascontiguousarray(cos2.reshape(32, 128, 64).transpose(1, 0, 2))
    c["sin2"] = np.ascontiguousarray(sin2.reshape(32, 128, 64).transpose(1, 0, 2))
    s_ = np.arange(128)[:, None]
    t_ = np.arange(128)[None, :]
    c["triinc"] = np.where(s_ <= t_, -1.0 / 16.0, 0.0).astype(np.float32)
    c["trirev"] = np.where(s_ > t_, -1.0 / 16.0, 0.0).astype(np.float32)
    c["mask01"] = np.where(s_ <= t_, 1.0, 0.0).astype(np.float32)
    return c


def build(n_chunks=NCHUNK, debug=False, stage=99):
    nc = bass.Bass("TRN2", target_bir_lowering=False)
    dt = nc.dram_tensor
    x = dt("x", [S, D], F32, kind="ExternalInput").ap()
    w_in = dt("w_in", [D, NIN], F32, kind="ExternalInput").ap()
    w_pa = dt("w_pa", [512, D], F32, kind="ExternalInput").ap()
    w_pb = dt("w_pb", [512, D], F32, kind="ExternalInput").ap()
    w_out = dt("w_out", [D, D], F32, kind="ExternalInput").ap()
    g_in = dt("g_in", [128, 8], F32, kind="ExternalInput").ap()
    g_f = dt("g_f", [1, D], F32, kind="ExternalInput").ap()
    g_gn = dt("g_gn", [1, 128], F32, kind="ExternalInput").ap()
    b_mg = dt("b_mg", [1, 2048], F32, kind="ExternalInput").ap()
    w_fg2 = dt("w_fg2", [16, 256], F32, kind="ExternalInput").ap()
    b_fg = dt("b_fg", [1, 256], F32, kind="ExternalInput").ap()
    d_ident = dt("c_ident", [128, 128], BF16, kind="ExternalInput").ap()
    d_cb = dt("c_cb", [128, 4, 512], BF16, kind="ExternalInput").ap()
    d_cos = dt("c_cos2", [128, 32, 64], F32, kind="ExternalInput").ap()
    d_sin = dt("c_sin2", [128, 32, 64], F32, kind="ExternalInput").ap()
    d_tri = dt("c_triinc", [128, 128], F32, kind="ExternalInput").ap()
    d_trr = dt("c_trirev", [128, 128], F32, kind="ExternalInput").ap()
    d_m01 = dt("c_mask01", [128, 128], F32, kind="ExternalInput").ap()
    out = dt("out", [S, D], F32, kind="ExternalOutput").ap()
    wsc = dt("wsc", [NSLOT, 128, 4096], BF16).ap()
    dbg_outs = {}

    es = contextlib.ExitStack()
    with es:
        def sb(name, shape, dtype):
            return es.enter_context(nc.sbuf_tensor(name, shape, dtype))

        def ps(name, shape, dtype=F32):
            return es.enter_context(nc.psum_tensor(name, shape, dtype))

        class Arena:
            def __init__(self, name, nelem):
                self.t = sb(name, [128, nelem], F32)
                self.off = 0
                self.n = nelem

            def reset(self):
                self.off = 0

            def f32(self, n, parts=128):
                v = self.t[0:parts, self.off:self.off + n]
                self.off += n
                assert self.off <= self.n, (self.off, self.n)
                return v

            def bf16(self, n, parts=128):
                assert n % 2 == 0
                v = self.t[0:parts, self.off:self.off + n // 2].bitcast(BF16)
                self.off += n // 2
                assert self.off <= self.n, (self.off, self.n)
                return v

        kT = sb("kT", [128, 4, S], BF16)
        vaug_t = sb("vaug", [128, 32 * 8 * 65], BF16)
        vaug = vaug_t[:, :].rearrange("p (t h d) -> p t h d", t=32, h=8)
        wb = [sb(f"wb{i}", [128, 4096], BF16) for i in range(NWB)]
        ident = sb("ident", [128, 128], BF16)
        cb = sb("cb", [128, 4, 512], BF16)
        cosc = sb("cosc", [128, 4, 64], F32)
        sinc = sb("sinc", [128, 4, 64], F32)
        gfbc = sb("gfbc", [128, D], F32)
        gngbc = sb("gngbc", [128, 128], F32)
        triinc = sb("triinc", [128, 128], F32)
        trirev = sb("trirev", [128, 128], F32)
        mask01 = sb("mask01", [128, 128], F32)
        c16 = sb("c16", [128, 1], F32)
        neghalf = sb("neghalf", [128, 1], F32)
        w2aug = sb("w2aug", [32, 256], F32)
        wfg = sb("wfg", [128, 8, 16], BF16)
        wfg32 = sb("wfg32", [128, 8, 16], F32)
        gin = sb("gin", [128, 8], F32)
        bm2 = sb("bm2", [33, 2048], BF16)
        ones2 = sb("ones2", [33, 128], BF16)
        S32 = sb("S32", [128, 2, 256], F32)
        Sbf = sb("Sbf", [128, 2, 256], BF16)
        kmT = sb("kmT", [128, 4, 16], BF16)
        kms = sb("kms", [128, 4, 2], F32)
        ybuf = sb("ybuf", [128, 4, D], F32)
        hT = sb("hT", [128, 8, TCH], BF16)
        junk = sb("junk", [128, D], BF16)
        ssx = sb("ssx", [128, 4], F32)
        rsx = sb("rsx", [128, 4], F32)
        dec = sb("dec", [128, 2], F32)
        ssg = sb("ssg", [128, 4], F32)
        rsg = sb("rsg", [128, 4], F32)
        ssf = sb("ssf", [128, 4], F32)
        rsf = sb("rsf", [128, 4], F32)
        rden = sb("rden", [128, 4, 1], F32)
        oa = sb("oa", [128, 4, 512], F32)
        ob = sb("ob", [128, 4, 512], F32)
        ar1 = sb("ar1", [128, 5120], F32)
        gqk = ar1[:, 0:2048].rearrange("p (j n) -> p j n", j=4)
        gv = ar1[:, 2048:3072].bitcast(BF16).rearrange("p (j n) -> p j n", j=4)
        qT = ar1[:, 3072:5120].bitcast(BF16).rearrange("p (h n) -> p h n", h=8)
        qT4 = ar1[:, 3072:5120].bitcast(BF16).rearrange("p (pr two n) -> p pr two n", pr=4, two=2)
        sgA = ar1[:, 0:2048].rearrange("p (j n) -> p j n", j=4)
        ar2 = Arena("ar2", 4096)
        hx = [ar2.bf16(D) for _ in range(2)]
        rot1 = [ar2.f32(512) for _ in range(2)]
        rot2 = [ar2.f32(512) for _ in range(2)]
        rotb = [ar2.bf16(512) for _ in range(2)]
        ar2.reset()
        mbf = ar2.bf16(4 * D).rearrange("p (j n) -> p j n", j=4)
        mT = ar2.bf16(8 * TCH).rearrange("p (k n) -> p k n", k=8)
        ar2_A = ["hx0", "hx1", "rot1_0", "rot1_1", "rot2_0", "rot2_1", "rotb0a", "rotb0b", "rotb1a", "rotb1b"]
        ar2_C = ["mbf0", "mbf1", "mbf2", "mbf3", "mT"]
        MBF = ["mbf0", "mbf1", "mbf2", "mbf3"]
        ar3 = Arena("ar3", 6656)
        gfgT = ar3.f32(TCH, parts=32)
        eneg = ar3.f32(256)
        lsp = ar3.f32(256)
        Eb = ar3.f32(512)
        enb = ar3.f32(256)
        qtl = ar3.bf16(256)
        ktl = ar3.bf16(256)
        khat = ar3.bf16(256)
        gqz_flat = ar3.bf16(512)
        gqz = gqz_flat.rearrange("p (h n) -> p h n", h=4)
        gqz4 = gqz_flat.rearrange("p (pr two n) -> p pr two n", pr=2, two=2)
        gkT = ar3.bf16(256).rearrange("p (k n) -> p k n", k=2)
        attn_sb = ar3.bf16(512).rearrange("p (k n) -> p k n", k=4)
        otmp = ar3.f32(512).rearrange("p (k n) -> p k n", k=4)
        pt = [ar3.bf16(1024) for _ in range(3)]
        gate_sb = ar3.f32(128).rearrange("p (h n) -> p h n", h=8)
        max8 = ar3.f32(64).rearrange("p (h n) -> p h n", h=8)
        sel = ar3.f32(512).rearrange("p (q h n) -> p q h n", q=4, h=8)
        acc = [ar3.f32(260).rearrange("p (q d) -> p q d", q=4) for _ in range(2)]
        pvt = [ar3.f32(260).rearrange("p (q d) -> p q d", q=4) for _ in range(2)]
        M8 = [f"max8_{h_}" for h_ in range(8)]
        GT3 = ["gTe", "gTo", "gTk"]
        OTG = [f"otmpg{h_}" for h_ in range(4)]
        ar3_B = ["gfgT", "eneg", "lsp", "Eb", "enb", "qtl", "ktl", "khat", "attn_sb", "otmp",
                 "pt0", "pt1", "pt2", "gate_sb", "sel", "acc0", "acc1", "pvt0", "pvt1"] + M8 + GT3 + OTG
        ar3.reset()
        sil = [ar3.f32(512) for _ in range(2)]
        ogb = [ar3.bf16(512) for _ in range(2)]
        mtmp = [ar3.f32(512) for _ in range(2)]
        oagT = ar3.bf16(4 * TCH).rearrange("p (k n) -> p k n", k=4)
        obgT = ar3.bf16(4 * TCH).rearrange("p (k n) -> p k n", k=4)
        sgB = ar3.f32(2048).rearrange("p (j n) -> p j n", j=4)
        SGA = [f"sgA{j_}" for j_ in range(4)]
        SGB = [f"sgB{j_}" for j_ in range(4)]
        ar3_C = ["sil0", "sil1", "ogb0", "ogb1", "mtmp0", "mtmp1", "oagT", "obgT"] + SGB
        psBig = ps("psBig", [128, 2048])
        psA = [psBig[:, 0:512], psBig[:, 512:1024]]
        psS = [psBig[:, 1024:1536], psBig[:, 1536:2048]]
        psT = [ps(f"psT{i}", [128, 1024], BF16) for i in range(2)]
        psV = [ps(f"psV{i}", [128, 512]) for i in range(2)]

        sc = Sched(nc)
        A = sc.add
        for n_ in SGA:
            sc.overlap("gqk", n_)
        for a_ in ar2_A:
            for b_ in ar2_C:
                sc.overlap(a_, b_)
        for a_ in ar3_B:
            for b_ in ar3_C:
                sc.overlap(a_, b_)

        ucnt = {"n": 0}

        def dma(outap, inap, reads, writes, sem, q="sp"):
            if sem in ("c0", "c1"):
                ucnt["n"] += 1
                sem = f"c{ucnt['n']}"
                if list(writes) != ["gin"]:
                    q = "act"
            return A(q, lambda e: e.dma_start(out=outap, in_=inap), reads, writes, dsem=sem)

        def mm(outap, lhsT, rhs, start, stop, reads, writes, skip=False):
            return A("pe", lambda e: e.matmul(outap, lhsT, rhs, start=start, stop=stop, skip_group_check=skip), reads, writes)

        def tr(outap, inap, reads, writes):
            return A("pe", lambda e: e.transpose(outap, inap, ident[:]), list(reads) + ["ident"], writes)

        dma(gin[:], g_in, [], ["gin"], "c0")
        bmf0 = oa[0:1, :, :]
        bmf32 = oa[32:33, :, :]
        sc.overlap("oa", "bmf0")
        sc.overlap("oa", "bmf32")
        dma(bmf0, b_mg.rearrange("o (j n) -> o j n", j=4), [], ["bmf0"], "c0")
        dma(bmf32, b_mg.rearrange("o (j n) -> o j n", j=4), [], ["bmf32"], "c0")
        A("dve", lambda e: e.memset(bm2[:], 0.0), [], ["bm2"])
        dma(ident[:], d_ident, [], ["ident"], "c0")
        dma(cb[:], d_cb, [], ["cb"], "c0")
        dma(triinc[:], d_tri, [], ["triinc"], "c0")
        dma(trirev[:], d_trr, [], ["trirev"], "c0")
        dma(mask01[:], d_m01, [], ["mask01"], "c0")
        dma(gfbc[:], g_f.partition_broadcast(128)[:, 0, :], [], ["gfbc"], "c0")
        dma(gngbc[:], g_gn.partition_broadcast(128)[:, 0, :], [], ["gngbc"], "c0")
        yb4 = ["ybuf0", "ybuf1", "ybuf2", "ybuf3"]
        A("dve", lambda e: e.memset(w2aug[:], 0.0), [], ["w2aug_a", "w2aug_b"])
        dma(w2aug[0:16, :], w_fg2, [], ["w2aug_a"], "c1")
        dma(w2aug[16:17, :], b_fg, [], ["w2aug_b"], "c1")
        dma(wfg32[:], w_in.rearrange("(kc p) n -> p kc n", p=128)[:, :, C_FG:C_FG + 16], [], ["wfg32"], "c1")
        A("dve", lambda e: e.memset(c16[:], -1.0 / 16.0), [], ["c16"])
        A("dve", lambda e: e.memset(neghalf[:], -0.5), [], ["neghalf"])
        A("dve", lambda e: e.memset(ones2[:], 1.0), [], ["ones2"])
        A("dve", lambda e: e.memset(gfgT[:], 1.0), [], ["gfgT"])
        A("dve", lambda e: e.memset(S32[:], 0.0), [], ["S32_0", "S32_1"])
        A("dve", lambda e: e.memset(Sbf[:], 0.0), [], ["Sbf"])
        A("dve", lambda e: e.memset(kmT[:], 0.0), [], ["kmT"])
        A("dve", lambda e: e.memset(vaug[:, 0:4, :, 64:65], 1.0), [], ["vones"])
        A("dve", lambda e: e.tensor_tensor(wfg[:], wfg32[:], gin[:].unsqueeze(2).broadcast_to([128, 8, 16]), ALU.mult),
          ["wfg32", "gin"], ["wfg"])
        def bm_split():
            bm4 = lambda r: bm2[r:r + 1, :].rearrange("o (j n) -> o j n", j=4)
            t32 = ar2.t[32:33, 0:2048].rearrange("o (j n) -> o j n", j=4)
            A("dve", lambda e: e.tensor_copy(bm4(0), bmf0), ["bmf0"], ["bm2"])
            A("dve", lambda e: e.tensor_copy(bm4(32), bmf32), ["bmf32"], ["bm2"])
            A("dve", lambda e: e.tensor_copy(t32, bm4(32)), ["bm2"], ar2_A)
            A("dve", lambda e: e.tensor_sub(t32, bmf32, t32), ["bmf32"] + ar2_A, ar2_A)
            A("dve", lambda e: e.tensor_copy(bm4(32), t32), ar2_A, ["bm2"])

        if stage == 0:
            srcs_skip = True
        vflat = vaug_t[:, 4 * 520:32 * 520]
        stgA = [vflat[:, 4096 * k_:4096 * (k_ + 1)].bitcast(F32) for k_ in range(3)]
        STG = ["stgA0", "stgA1", "stgA2"]
        for gt_ in range(4, 32):
            for r_ in STG:
                sc.overlap(f"v{gt_}", r_)
        w_in_v = w_in.rearrange("(kc p) n -> p kc n", p=128)
        w_pa_v = w_pa.rearrange("(kc p) n -> p kc n", p=128)
        w_pb_v = w_pb.rearrange("(kc p) n -> p kc n", p=128)
        w_out_v = w_out.rearrange("(kc p) n -> p kc n", p=128)
        src_of = {}
        for sl in range(11):
            v_ = w_in_v[:, :, SL_G[sl]:SL_G[sl] + 512]
            src_of[sl] = ([v_[:, 0:4, :], v_[:, 4:8, :]], True)
        src_of[SL_PAB0] = ([w_pa_v[:, :, 0:512], w_pb_v[:, :, 0:512]], False)
        src_of[SL_PAB1] = ([w_pa_v[:, :, 512:1024], w_pb_v[:, :, 512:1024]], False)
        src_of[SL_WO0] = ([w_out_v[:, 0:4, 0:512], w_out_v[:, 4:8, 0:512]], False)
        src_of[SL_WO1] = ([w_out_v[:, 0:4, 512:1024], w_out_v[:, 4:8, 512:1024]], False)
        units = []
        for i_, sl_ in enumerate(CHUNK_ORDER if stage != 0 else []):
            for hf_ in range(2):
                units.append((i_, sl_, hf_))
        ustate = {"loaded": 0, "cast": 0}

        def unit_load(u):
            i, sl, hf = units[u]
            b = u % 3
            a32 = stgA[b].rearrange("p (k n) -> p k n", k=4)
            dma(a32, src_of[sl][0][hf], [], [f"stgA{b}"], f"ws{b}")

        def unit_cast(u):
            i, sl, hf = units[u]
            b = u % 3
            bw = i % NWB
            a32 = stgA[b].rearrange("p (k n) -> p k n", k=4)
            dst = wb[bw][:, hf * 2048:(hf + 1) * 2048].rearrange("p (k n) -> p k n", k=4)
            if hf == 0:
                if src_of[sl][1]:
                    A("dve", lambda e: e.tensor_tensor(
                        dst, a32, gin[:, 0:4].unsqueeze(2).broadcast_to([128, 4, 512]), ALU.mult),
                      [f"stgA{b}", "gin"], [f"wb{bw}"])
                else:
                    A("dve", lambda e: e.tensor_copy(dst, a32), [f"stgA{b}"], [f"wb{bw}"])
            else:
                if src_of[sl][1]:
                    for kc in range(4):
                        A("act", lambda e, kc=kc: e.activation(dst[:, kc, :], a32[:, kc, :], AF.Copy, scale=gin[:, 4 + kc:5 + kc]),
                          [f"stgA{b}", "gin"], [f"wb{bw}"])
                else:
                    A("act", lambda e: e.copy(dst, a32), [f"stgA{b}"], [f"wb{bw}"])
            if hf == 1:
                dma(wsc[sl, :, :], wb[bw][:, :], [f"wb{bw}"], [f"wsc{sl}"], f"wt{bw}", q="act")

        def cast_tile_into(i, sl):
            while ustate["cast"] < len(units) and units[ustate["cast"]][0] <= i:
                u = ustate["cast"]
                while ustate["loaded"] <= u:
                    unit_load(ustate["loaded"])
                    ustate["loaded"] += 1
                unit_cast(u)
                ustate["cast"] += 1
                while ustate["loaded"] < min(len(units), ustate["cast"] + 3):
                    unit_load(ustate["loaded"])
                    ustate["loaded"] += 1

        wseq = [s_ for _ in range(n_chunks) for s_ in CHUNK_ORDER]
        wstate = {"next": 0}

        def wload(i):
            sl = wseq[i]
            n = 4096
            if i < len(CHUNK_ORDER) and stage != 0:
                cast_tile_into(i, sl)
            else:
                dma(wb[i % NWB][:, 0:n], wsc[sl, :, 0:n], [f"wsc{sl}"], [f"wb{i % NWB}"], f"wb{i % NWB}")

        def getw(i):
            while wstate["next"] < min(len(wseq), i + NWB):
                wload(wstate["next"])
                wstate["next"] += 1
            return wb[i % NWB], f"wb{i % NWB}"

        witer = {"i": 0}

        def nextw(expect):
            i = witer["i"]
            assert wseq[i] == expect, (i, wseq[i], expect)
            witer["i"] += 1
            return getw(i)

        def inproj(wt, wres, j, pbank, pres, ncols=512):
            wv = wt[:, :].rearrange("p (k n) -> p k n", k=8)
            for kc in range(8):
                mm(pbank[:, 0:ncols], hT[:, kc, j * 128:(j + 1) * 128], wv[:, kc, 0:ncols],
                   kc == 0, kc == 7, [f"hT{j}", wres], [pres])

        acount = {"n": 0}

        A3 = [(psA[0], "psA0"), (psA[1], "psA1"), (psV[0], "psV0")]

        def nextA():
            i = acount["n"] % 3
            acount["n"] += 1
            return A3[i]

        class _Stop(Exception):
            pass

        def dump(name, ap, reads):
            d = nc.dram_tensor("dbg_" + name, list(ap.shape), ap.dtype, kind="ExternalOutput").ap()
            dma(d, ap, reads, [], "dbg_" + name)
            dbg_sems.append("dbg_" + name)

        dbg_sems = []

        curc = {"c": 0}

        def chk(k, dumps):
            if stage == k and (k < 2 or curc["c"] == int(os.environ.get("STOPC", "0"))):
                for name, ap, reads in dumps:
                    dump(name, ap, reads)
                raise _Stop()

        def pipeline(tasks, depth=2):
            q_ = []
            for main, post in tasks:
                main()
                q_.append(post)
                if len(q_) > depth:
                    q_.pop(0)()
            while q_:
                q_.pop(0)()

        tcount = {"n": 0}
        prefetched = {"hT": False}
        HT4 = ["hT0", "hT1", "hT2", "hT3"]

        T3 = [(psT[0], "psT0"), (psT[1], "psT1"), (psV[1].bitcast(BF16), "psV1")]

        def nextT():
            i = tcount["n"] % 3
            tcount["n"] += 1
            return T3[i]

        try:
          chk(0, [("bm2", bm2[:], ["bm2"]), ("wfg", wfg[:], ["wfg"]), ("gfbc", gfbc[:], ["gfbc"]), ("w2aug", w2aug[:], ["w2aug_a", "w2aug_b"])])
          chk(1, [("bm2", bm2[:], ["bm2"]), ("wfg", wfg[:], ["wfg"])])
          for c in range(n_chunks):
              curc["c"] = c
              t0 = c * TCH
              if c == 1:
                  A("pool", lambda e: e.memset(vaug[:, 4:32, :, 64:65], 1.0), [], ["vones"] + STG)
              if not prefetched["hT"]:
                  dma(cosc[:], d_cos[:, 4 * c:4 * c + 4, :], [], ["cosc"], "cos")
                  dma(sinc[:], d_sin[:, 4 * c:4 * c + 4, :], [], ["sinc"], "sin")
              oa2 = oa[:].rearrange("p j n -> p (j n)")
              ob2 = ob[:].rearrange("p j n -> p (j n)")
              xs_pref = [(oa2[:, 0:1024], "oa"), (oa2[:, 1024:2048], "oa"), (ob2[:, 0:1024], "ob"), (ob2[:, 1024:2048], "ob")]
              sqd = sil[0].bitcast(BF16)

              def a1_stats(cc, j, xsrc, xres, dummy, dres, load):
                  gt = 4 * cc + j
                  A("act", lambda e: e.activation(dummy, xsrc, AF.Square, accum_out=ssx[:, j:j + 1]),
                    [xres], [dres, f"ssx{j}"])
                  A("dve", lambda e: e.tensor_scalar(rsx[:, j:j + 1], ssx[:, j:j + 1], 1.0 / D, EPS, ALU.mult, ALU.add),
                    [f"ssx{j}"], [f"rsx{j}"])
                  A("pool", lambda e: e.tensor_tensor(rsx[:, j:j + 1], rsx[:, j:j + 1], neghalf[:], ALU.pow),
                    [f"rsx{j}", "neghalf"], [f"rsx{j}"])

              def a1_hx_act(j, xsrc, xres, hb, hres):
                  A("act", lambda e: e.activation(hb[:], xsrc, AF.Copy, scale=rsx[:, j:j + 1]), [xres, f"rsx{j}"], [hres])

              def a1_task(j, xsrc, xres, hb, hres, do_hx=True, post_extra=None):
                  st_ = {}
                  def main():
                      pT, tres = st_["t"] = nextT()
                      if do_hx:
                          A("dve", lambda e: e.tensor_scalar(hb[:], xsrc, rsx[:, j:j + 1], None, ALU.mult),
                            [xres, f"rsx{j}"], [hres])
                      for kc in range(8):
                          tr(pT[:, kc * 128:(kc + 1) * 128], hb[:, kc * 128:(kc + 1) * 128], [hres], [tres])
                  def post():
                      pT, tres = st_["t"]
                      A("act", lambda e: e.copy(hT[:, :, j * 128:(j + 1) * 128], pT[:, :].rearrange("p (k n) -> p k n", k=8)),
                        [tres], [f"hT{j}"])
                      if post_extra is not None:
                          post_extra()
                  return (main, post)

              def load_x_resid():
                  for j in range(4):
                      gt = 4 * c + j
                      dma(ybuf[:, j, :], x[gt * 128:(gt + 1) * 128, :], [], [f"ybuf{j}"], f"x{j}")

              x_early = not prefetched["hT"]
              if x_early:
                  load_x_resid()
              tasks = []
              if not prefetched["hT"]:
                  for j in range(4):
                      a1_stats(c, j, ybuf[:, j, :], f"ybuf{j}", junk[:], "junk", False)
                  tasks = [a1_task(j, ybuf[:, j, :], f"ybuf{j}", hx[j % 2], f"hx{j % 2}") for j in range(4)]
              if c == 0:
                  A("pool", lambda e: e.memset(qT4[0:64, :, 1, :], 0.0), [], ["qT"])
                  A("pool", lambda e: e.memset(qT4[64:128, :, 0, :], 0.0), [], ["qT"])
              a1_tasks = tasks
              if stage == 2:
                  pipeline(tasks)
              chk(2, [("hT", hT[:], HT4), ("rsx", rsx[:], [f"rsx{j}" for j in range(4)])])

              def gla_tile(j, out_list, G0, G1, T0, n0, n1):
                  S_ = out_list.append

                  def s0():
                      mm(G0[:, 0:256], gfgT[:, j * 128:(j + 1) * 128], w2aug[:, :], True, True, ["gfgT", "w2aug_a", "w2aug_b"], [n0])
                  S_(s0)

                  def s1():
                      A("act", lambda e: e.activation(eneg[:], G0[:, 0:256], AF.Exp, scale=-1.0), [n0], ["eneg"])
                      A("act", lambda e: e.activation(lsp[:], eneg[:], AF.Ln, bias=1.0), ["eneg"], ["lsp"])
                  S_(s1)

                  def s2():
                      mm(G1[:, 0:256], triinc[:], lsp[:], True, True, ["triinc", "lsp"], [n1])
                      mm(G1[:, 256:512], trirev[:], lsp[:], True, True, ["trirev", "lsp"], [n1])
                      for pr in range(2):
                          mm(G0[:, 256 + pr:257 + pr], lsp[:, pr * 128:(pr + 1) * 128], c16[:], True, True,
                             ["lsp", "c16"], [n0])
                  S_(s2)

                  def s3():
                      A("act", lambda e: e.activation(Eb[:], G1[:, :], AF.Exp), [n1], ["Eb"])
                      A("act", lambda e: e.activation(enb[:], G1[:, 0:256], AF.Exp, scale=-1.0), [n1], ["enb"])
                      A("act", lambda e: e.activation(dec[:], G0[:, 256:258], AF.Exp), [n0], ["dec"])
                  S_(s3)

                  def s4():
                      A("dve", lambda e: e.scalar_tensor_tensor(qtl[:], gqk[:, j, 0:256], 0.125, Eb[:, 0:256], ALU.mult, ALU.mult),
                        ["gqk", "Eb"], ["qtl"])
                      A("dve", lambda e: e.tensor_mul(ktl[:], gqk[:, j, 256:512], enb[:]), ["gqk", "enb"], ["ktl"])
                      A("pool", lambda e: e.tensor_mul(khat[:], gqk[:, j, 256:512], Eb[:, 256:512]), ["gqk", "Eb"], ["khat"])
                      if j == 0:
                          A("pool", lambda e: e.memset(gqz_flat, 0.0), [], ["gTe", "gTo"])
                  S_(s4)

                  def s5():
                      for pr in range(2):
                          tr(T0[:, pr * 128:(pr + 1) * 128], qtl[:, pr * 128:(pr + 1) * 128], ["qtl"], [n0])
                      for pr in range(2):
                          tr(T0[:, (2 + pr) * 128:(3 + pr) * 128], ktl[:, pr * 128:(pr + 1) * 128], ["ktl"], [n0])
                  S_(s5)

                  def s6():
                      pq = T0[:, 0:256].rearrange("p (k n) -> p k n", k=2)
                      A("dve", lambda e: e.tensor_copy(gqz4[0:64, :, 0, :], pq[0:64]), [n0], ["gTe"])
                      A("dve", lambda e: e.tensor_copy(gqz4[64:128, :, 1, :], pq[64:128]), [n0], ["gTo"])
                      A("dve", lambda e: e.tensor_copy(gkT, T0[:, 256:512].rearrange("p (k n) -> p k n", k=2)), [n0], ["gTk"])
                  S_(s6)

                  def s7():
                      for hg in range(4):
                          mm(G1[:, hg * 128:(hg + 1) * 128], gkT[:, hg // 2, :], gqz[:, hg, :], True, True, GT3, [n1])
                  S_(s7)

                  def s8():
                      A("dve", lambda e: e.tensor_tensor(
                          attn_sb[:], G1[:, :].rearrange("p (h n) -> p h n", h=4),
                          mask01[:].unsqueeze(1).broadcast_to([128, 4, 128]), ALU.mult),
                        [n1, "mask01"], ["attn_sb"])
                  S_(s8)

                  def s9():
                      for hg in range(4):
                          pr = hg // 2
                          mm(G0[:, hg * 128:(hg + 1) * 128], attn_sb[:, hg, :], gv[:, j, hg * 128:(hg + 1) * 128],
                             True, False, ["attn_sb", "gv"], [n0])
                          mm(G0[:, hg * 128:(hg + 1) * 128], gqz[:, hg, :],
                             Sbf[:, pr, (hg % 2) * 128:(hg % 2) * 128 + 128],
                             False, True, ["gTe", "gTo", "Sbf"], [n0])
                      for pr in range(2):
                          mm(G1[:, pr * 256:(pr + 1) * 256], khat[:, pr * 128:(pr + 1) * 128],
                             gv[:, j, pr * 256:(pr + 1) * 256], True, True, ["khat", "gv"], [n1])
                  S_(s9)

                  def s10():
                      for pr in range(2):
                          A("dve", lambda e, pr=pr: e.scalar_tensor_tensor(
                              S32[:, pr, :], S32[:, pr, :], dec[:, pr:pr + 1], G1[:, pr * 256:(pr + 1) * 256],
                              ALU.mult, ALU.add), [f"S32_{pr}", "dec", n1], [f"S32_{pr}"])
                      A("dve", lambda e: e.tensor_copy(Sbf[:], S32[:]), ["S32_0", "S32_1"], ["Sbf"])
                      for hg in range(4):
                          A("act", lambda e, hg=hg: e.activation(otmp[:, hg, :], G0[:, hg * 128:(hg + 1) * 128], AF.Square,
                                                                 accum_out=ssg[:, hg:hg + 1]), [n0], [f"otmpg{hg}", f"ssg{hg}"])
                  S_(s10)

                  def s11():
                      A("dve", lambda e: e.tensor_scalar(rsg[:], ssg[:], 1.0 / 128.0, EPS, ALU.mult, ALU.add), [f"ssg{h_}" for h_ in range(4)], ["rsg"])
                      A("pool", lambda e: e.tensor_tensor(rsg[:], rsg[:], neghalf[:].broadcast_to([128, 4]), ALU.pow),
                        ["rsg", "neghalf"], ["rsg"])
                  S_(s11)

                  def s12():
                      A("dve", lambda e: e.tensor_tensor(
                          otmp[:], G0[:, :].rearrange("p (h n) -> p h n", h=4),
                          rsg[:].unsqueeze(2).broadcast_to([128, 4, 128]), ALU.mult), [n0, "rsg"], ["otmp"] + OTG)
                      A("pool", lambda e: e.tensor_tensor(
                          ob[:, j, :].rearrange("p (h n) -> p h n", h=4), otmp[:],
                          gngbc[:].unsqueeze(1).broadcast_to([128, 4, 128]), ALU.mult), ["otmp", "gngbc"] + OTG, ["ob"])
                  S_(s12)

              tasks = []
              for sl in (0, 1):
                  for j in range(4):
                      st_ = {}
                      def main(sl=sl, j=j, st_=st_):
                          if j == 0:
                              st_["w"] = nextw(sl)
                              wcur[sl] = st_["w"]
                          wt, wres = wcur[sl]
                          st_["p"] = nextA()
                          inproj(wt, wres, j, *st_["p"])
                      def post(sl=sl, j=j, st_=st_):
                          gt = 4 * c + j
                          pb_, pres = st_["p"]
                          k2 = (sl * 4 + j) % 2
                          r1, r2, rb = rot1[k2], rot2[k2], rotb[k2]
                          p3 = pb_[:, :].rearrange("p (h d) -> p h d", h=8)
                          cos_b = cosc[:, j, :].unsqueeze(1).broadcast_to([128, 8, 64])
                          sin_b = sinc[:, j, :].unsqueeze(1).broadcast_to([128, 8, 64])
                          r13 = r1[:, :].rearrange("p (h d) -> p h d", h=8)
                          r23 = r2[:, :].rearrange("p (h d) -> p h d", h=8)
                          rb3 = rb[:, :].rearrange("p (h d) -> p h d", h=8)
                          A("dve", lambda e: e.tensor_tensor(r13, p3, cos_b, ALU.mult), [pres, "cosc"], [f"rot1_{k2}"])
                          A("dve", lambda e: e.tensor_tensor(r23, p3, sin_b, ALU.mult), [pres, "sinc"], [f"rot2_{k2}"])
                          A("pool", lambda e: e.tensor_sub(rb3[:, :, 0:32], r13[:, :, 0:32], r23[:, :, 32:64]),
                            [f"rot1_{k2}", f"rot2_{k2}"], [f"rotb{k2}a"])
                          A("pool", lambda e: e.tensor_add(rb3[:, :, 32:64], r13[:, :, 32:64], r23[:, :, 0:32]),
                            [f"rot1_{k2}", f"rot2_{k2}"], [f"rotb{k2}b"])
                          pT, tres = nextT()
                          for pr in range(4):
                              tr(pT[:, pr * 128:(pr + 1) * 128], rb[:, pr * 128:(pr + 1) * 128], [f"rotb{k2}a", f"rotb{k2}b"], [tres])
                          src = pT[:, 0:512].rearrange("p (k n) -> p k n", k=4)
                          if sl == 0:
                              A("act", lambda e: e.copy(qT4[0:64, :, 0, j * 128:(j + 1) * 128], src[0:64]), [tres], ["qT"])
                              A("act", lambda e: e.copy(qT4[64:128, :, 1, j * 128:(j + 1) * 128], src[64:128]), [tres], ["qT"])
                          else:
                              A("act", lambda e: e.copy(kT[:, :, gt * 128:(gt + 1) * 128], src), [tres], [f"kT{gt}"])
                      tasks.append((main, post))
              wcur = {}
              qk_tasks = tasks
              tasks = []
              for sl, kind in ((4, "gqk"), (5, "gv"), (2, "v")):
                  for j in range(4):
                      st_ = {}
                      def main(sl=sl, j=j, st_=st_):
                          if j == 0:
                              wcur[sl] = nextw(sl)
                          wt, wres = wcur[sl]
                          st_["p"] = nextA()
                          inproj(wt, wres, j, *st_["p"])
                      def post(kind=kind, j=j, st_=st_):
                          gt = 4 * c + j
                          pb_, pres = st_["p"]
                          if kind == "v":
                              A("act", lambda e: e.copy(vaug[:, gt, :, 0:64], pb_[:, :].rearrange("p (h d) -> p h d", h=8)),
                                [pres], [f"v{gt}"])
                          elif kind == "gqk":
                              A("dve", lambda e: e.tensor_copy(gqk[:, j, :], pb_[:, :]), [pres], ["gqk"])
                          else:
                              A("act", lambda e: e.copy(gv[:, j, :], pb_[:, :]), [pres], ["gv"])
                      tasks.append((main, post))
              g_tasks, v_tasks = tasks[0:8], tasks[8:12]
              st_g = {}
              def main_g():
                  st_g["p"] = nextA()
                  pb_, pres = st_g["p"]
                  for kc in range(8):
                      mm(pb_[0:16, :], wfg[:, kc, :], hT[:, kc, :], kc == 0, kc == 7, HT4 + ["wfg"], [pres])
              def post_g():
                  pb_, pres = st_g["p"]
                  A("pool", lambda e: e.memset(gfgT, 1.0), [], ["gfgT"])
                  A("act", lambda e: e.copy(gfgT[0:16, :], pb_[0:16, :]), [pres], ["gfgT"])
              a2 = g_tasks + [(main_g, post_g)] + qk_tasks + v_tasks
              tasks = a2
              if a1_tasks:
                  tasks = [a1_tasks[0], a1_tasks[1], a1_tasks[2], a2[0], a1_tasks[3]] + a2[1:]
              def kmean_compute():
                  A("dve", lambda e, t0=t0: e.tensor_reduce(
                      kms[:], kT[:, :, t0:t0 + TCH].rearrange("p k (b n) -> p k b n", b=2), AX.X, ALU.add),
                    [f"kT{4 * c + j}" for j in range(4)], ["kms"])
                  A("dve", lambda e, c=c: e.tensor_scalar(kmT[:, :, 2 * c:2 * c + 2], kms[:], 1.0 / 256.0, None, ALU.mult),
                    ["kms"], ["kmT"])
                  A("dve", lambda e: e.memset(sel[:], 0.0), [], ["sel"])

              def sel_compute(qi):
                  own = 2 * c + qi // 2
                  if own <= 3:
                      A("dve", lambda e, qi=qi, own=own: e.memset(sel[:, qi, :, 0:own + 1], 1.0), [], ["sel"])
                      return
                  pG, gres = psV[1], "psV1"
                  for h in range(8):
                      mm(pG[:, h * 16:(h + 1) * 16], qT[:, h, qi * 128:(qi + 1) * 128],
                         kmT[:, h // 2, :], True, True, ["qT", "kmT"], [gres])
                  A("dve", lambda e, pG=pG: e.tensor_copy(gate_sb[:], pG[:, 0:128].rearrange("p (h n) -> p h n", h=8)),
                    [gres], ["gate_sb"])
                  if own < 16:
                      A("dve", lambda e, own=own: e.memset(gate_sb[:, :, own:16], -1e30), [], ["gate_sb"])
                  for h in range(8):
                      A("dve", lambda e, h=h: e.max(max8[:, h, :], gate_sb[:, h, :]), ["gate_sb"], [f"max8_{h}"])
                  A("dve", lambda e, qi=qi: e.tensor_tensor(
                      sel[:, qi, :, :], gate_sb[:], max8[:, :, 2:3].broadcast_to([128, 8, 16]), ALU.is_ge),
                    ["gate_sb"] + M8, ["sel"])
                  A("dve", lambda e, qi=qi, own=own: e.memset(sel[:, qi, :, own:own + 1], 1.0), [], ["sel"])


              early_gla = []
              n_early = (2 if c <= 3 else (1 if c <= 5 else 0)) if stage > 90 else 0
              if n_early:
                  for j in range(n_early):
                      gla_tile(j, early_gla, psS[0], psS[1], psS[0].bitcast(BF16), "psS0", "psS1")
                  ne = len(early_gla)
                  base = len(tasks) - 12
                  for t_ in range(12):
                      lo, hi = (t_ * ne) // 12, ((t_ + 1) * ne) // 12
                      m_, p_ = tasks[base + t_]
                      tasks[base + t_] = (m_, (lambda p_=p_, lo=lo, hi=hi: (p_(), [early_gla[q_]() for q_ in range(lo, hi)])))
              hooks = [[kmean_compute], [lambda: sel_compute(0)], [lambda: sel_compute(1)],
                       [lambda: sel_compute(2), lambda: sel_compute(3)]]
              for t_ in range(4):
                  m_, p_ = tasks[len(tasks) - 4 + t_]
                  tasks[len(tasks) - 4 + t_] = (m_, (lambda p_=p_, hk=hooks[t_]: (p_(), [h_() for h_ in hk])))
              pipeline(tasks)
              if c == 0:
                  bm_split()
              chk(3, [("qT", qT, ["qT"]), ("kT", kT[:, :, 0:512], [f"kT{j}" for j in range(4)]), ("v", vaug[:, 0:4], [f"v{j}" for j in range(4)]), ("kmT", kmT[:], ["kmT"])])
              chk(4, [("gqk", gqk, ["gqk"]), ("gv", gv, ["gv"]), ("gfgT", gfgT, ["gfgT"])])

              gla_stages = []
              G0f, G1f = psT[0][:, :].bitcast(F32), psT[1][:, :].bitcast(F32)
              for j in range(n_early, 4):
                  gla_tile(j, gla_stages, G0f, G1f, psT[0], "psT0", "psT1")
              if stage == 5:
                  for s_ in gla_stages:
                      s_()
                  gla_stages = []
              chk(5, [("ob", ob[:], ["ob"]), ("S32", S32[:], ["S32_0", "S32_1"]), ("lsp", lsp, ["lsp"]), ("Eb", Eb, ["Eb"])])

              nblk = 2 * c + 2
              mtasks = [(h, n) for h in range(8) for n in range(nblk)]
              SREG = [(psBig[:, 0:1024], ["psA0", "psA1"]), (psBig[:, 1024:2048], ["psS0", "psS1"])]

              def m_ST(idx):
                  h, n = mtasks[idx]
                  region, rres = SREG[idx % 2]
                  inchunk = n >= 2 * c
                  q0 = 256 if n == 2 * c + 1 else 0
                  for kk in range(2):
                      kt = 2 * n + kk
                      jk = kt - 4 * c
                      mm(region[:, kk * 512 + q0:(kk + 1) * 512], kT[:, h // 2, kt * 128:(kt + 1) * 128], qT[:, h, q0:512],
                         True, not inchunk, [f"kT{kt}", "qT"], [rres[kk]])
                      if inchunk:
                          mm(region[:, kk * 512 + q0:(kk + 1) * 512], ident[:], cb[:, jk, q0:512], False, True,
                             ["ident", "cb"], [rres[kk]])

              def m_exp(idx):
                  region, rres = SREG[idx % 2]
                  pbuf, pres_ = pt[idx % 3], f"pt{idx % 3}"
                  if mtasks[idx][1] == 2 * c + 1:
                      src = region.rearrange("p (k n) -> p k n", k=2)[:, :, 256:512]
                      dst = pbuf.rearrange("p (k n) -> p k n", k=2)[:, :, 256:512]
                      A("act", lambda e: e.activation(dst, src, AF.Exp, scale=0.125), rres, [pres_])
                  else:
                      A("act", lambda e: e.activation(pbuf[:], region, AF.Exp, scale=0.125), rres, [pres_])

              def m_rest(idx):
                  h, n = mtasks[idx]
                  region, rres = SREG[idx % 2]
                  pbuf, pres_ = pt[idx % 3], f"pt{idx % 3}"
                  vb = idx % 2
                  aE, aO = acc[0], acc[1]
                  inchunk = n >= 2 * c
                  qlo = 0 if n <= 2 * c else 2
                  round_first = True
                  for kk in range(2):
                      kt = 2 * n + kk
                      jk = kt - 4 * c
                      for qi in range(qlo, 4):
                          if inchunk and qi < jk:
                              continue
                          last = (kk == 1) or (inchunk and qi == jk)
                          mm(psV[vb][:, qi * 65:(qi + 1) * 65], pbuf[:, kk * 512 + qi * 128:kk * 512 + (qi + 1) * 128],
                             vaug[:, kt, h, :], round_first, last, [pres_, f"v{kt}", "vones"], [f"psV{vb}"], skip=True)
                          round_first = False
                  pv3 = psV[vb][:, 0:260].rearrange("p (q d) -> p q d", q=4)
                  selv = sel[:, qlo:4, h, n:n + 1].broadcast_to([128, 4 - qlo, 65])
                  if n == 0:
                      A("dve", lambda e: e.tensor_tensor(aE[:], pv3, selv, ALU.mult), [f"psV{vb}", "sel"], ["acc0"])
                  elif n == 1:
                      A("dve", lambda e: e.tensor_tensor(aO[:, qlo:4, :], pv3[:, qlo:4, :], selv, ALU.mult),
                        [f"psV{vb}", "sel"], ["acc1"])
                      if qlo > 0:
                          A("dve", lambda e: e.memset(aO[:, 0:qlo, :], 0.0), [], ["acc1"])
                  else:
                      tb = pvt[vb]
                      A("dve", lambda e: e.tensor_tensor(tb[:, qlo:4, :], pv3[:, qlo:4, :], selv, ALU.mult),
                        [f"psV{vb}", "sel"], [f"pvt{vb}"])
                      if n % 2 == 0:
                          A("pool", lambda e: e.tensor_add(aE[:, qlo:4, :], aE[:, qlo:4, :], tb[:, qlo:4, :]),
                            [f"pvt{vb}", "acc0"], ["acc0"])
                      else:
                          A("dve", lambda e: e.tensor_add(aO[:, qlo:4, :], aO[:, qlo:4, :], tb[:, qlo:4, :]),
                            [f"pvt{vb}", "acc1"], ["acc1"])
                  if n == nblk - 1:
                      A("dve", lambda e: e.tensor_add(aE[:], aE[:], aO[:]), ["acc0", "acc1"], ["acc0"])
                      A("dve", lambda e: e.reciprocal(rden[:], aE[:, :, 64:65]), ["acc0"], ["rden"])
                      A("dve", lambda e: e.tensor_tensor(oa[:, :, h * 64:(h + 1) * 64], aE[:, :, 0:64],
                                                         rden[:].broadcast_to([128, 4, 64]), ALU.mult),
                        ["acc0", "rden"], ["oa"])

              nmt = len(mtasks)
              gq_ = list(gla_stages)
              ngs = len(gq_)

              m_ST(0)
              if nmt > 1:
                  m_ST(1)
              for i in range(nmt):
                  m_exp(i)
                  if i + 2 < nmt:
                      m_ST(i + 2)
                  m_rest(i)
                  while gq_ and (ngs - len(gq_)) < ((i + 1) * ngs + nmt - 1) // nmt:
                      gq_.pop(0)()
              while gq_:
                  gq_.pop(0)()
              chk(6, [("oa", oa[:], ["oa"]), ("sel", sel, ["sel"])])

              if not x_early:
                  load_x_resid()
              tasks = []
              wcur = {}

              def add_gated(slot, src, srcres, dstT, dres):
                  for j in range(4):
                      st_ = {}
                      def main(j=j, st_=st_):
                          if j == 0:
                              wcur[slot] = nextw(slot)
                          wt, wres = wcur[slot]
                          st_["p"] = nextA()
                          inproj(wt, wres, j, *st_["p"])
                      def post(j=j, st_=st_):
                          pb_, pres = st_["p"]
                          sb_, sres = sil[j % 2], f"sil{j % 2}"
                          gb_, gres = ogb[j % 2], f"ogb{j % 2}"
                          A("act", lambda e: e.activation(sb_[:], pb_[:, :], AF.Silu), [pres], [sres])
                          A("dve", lambda e: e.tensor_mul(gb_[:], src[:, j, :], sb_[:]), [sres, srcres], [gres])
                          pT, tres = nextT()
                          for fc in range(4):
                              tr(pT[:, fc * 128:(fc + 1) * 128], gb_[:, fc * 128:(fc + 1) * 128], [gres], [tres])
                          A("act", lambda e: e.copy(dstT[:, :, j * 128:(j + 1) * 128],
                                                    pT[:, 0:512].rearrange("p (k n) -> p k n", k=4)), [tres], [dres])
                      tasks.append((main, post))

              def add_gate_half(slot, dst, dres, bcol):
                  for j in range(4):
                      st_ = {}
                      def main(j=j, st_=st_):
                          if j == 0:
                              wcur[slot] = nextw(slot)
                          wt, wres = wcur[slot]
                          pb_, pres = st_["p"] = nextA()
                          wv = wt[:, :].rearrange("p (k n) -> p k n", k=8)
                          for kc in range(8):
                              mm(pb_[:, :], hT[:, kc, j * 128:(j + 1) * 128], wv[:, kc, :], kc == 0, False, [f"hT{j}", wres], [pres])
                          mm(pb_[:, :], ones2[:, :], bm2[:, bcol:bcol + 512], False, True, ["ones2", "bm2"], [pres])
                      def post(j=j, st_=st_):
                          pb_, pres = st_["p"]
                          A("act", lambda e: e.activation(dst[:, j, :], pb_[:, :], AF.Sigmoid), [pres], [f"{dres}{j}"])
                      tasks.append((main, post))

              def add_proj_half(slot, srcT, sres, is_b, half):
                  for j in range(4):
                      st_ = {}
                      def main(j=j, st_=st_):
                          if j == 0 and not is_b:
                              wcur[slot] = nextw(slot)
                          wt, wres = wcur[slot]
                          wv = (wt[:, 2048:4096] if is_b else wt[:, 0:2048]).rearrange("p (k n) -> p k n", k=4)
                          pb_, pres = st_["p"] = nextA()
                          for kc in range(4):
                              mm(pb_[:, :], srcT[:, kc, j * 128:(j + 1) * 128], wv[:, kc, :], kc == 0, kc == 3, [sres, wres], [pres])
                      def post(j=j, st_=st_):
                          pb_, pres = st_["p"]
                          if not is_b:
                              A("dve", lambda e: e.tensor_mul(sgA[:, j, :], sgA[:, j, :], pb_[:, :]), [f"sgA{j}", pres], [f"sgA{j}"])
                          else:
                              mt, mres = mtmp[j % 2], f"mtmp{j % 2}"
                              A("dve", lambda e: e.tensor_mul(mt[:], sgB[:, j, :], pb_[:, :]), [f"sgB{j}", pres], [mres])
                              A("pool", lambda e: e.tensor_add(mbf[:, j, half * 512:(half + 1) * 512], sgA[:, j, :], mt[:]),
                                [f"sgA{j}", mres], [f"mbf{j}"])
                      tasks.append((main, post))

              do_pref = (c + 1 < n_chunks) and stage > 90
              a1p = []
              for half, (sga_sl, sgb_sl, pa_sl, pb_sl) in enumerate(((7, 9, SL_PAB0, SL_PAB0), (8, 10, SL_PAB1, SL_PAB1))):
                  n9 = len(tasks)
                  add_gate_half(sga_sl, sgA, "sgA", half * 512)
                  if half == 1 and do_pref:
                      for j in range(4):
                          m_, p_ = tasks[n9 + j]
                          tasks[n9 + j] = (m_, (lambda p_=p_, j=j: (p_(), a1_stats(c + 1, j, xs_pref[j][0], xs_pref[j][1], sqd, "sil0", False))))
                  n0 = len(tasks)
                  add_gate_half(sgb_sl, sgB, "sgB", 1024 + half * 512)
                  if half == 0:
                      add_gated(3, oa, "oa", oagT, "oagT")
                      add_gated(6, ob, "ob", obgT, "obgT")
                  if half == 1 and do_pref:
                      hxb = [(junk, "junk"), (sil[1].bitcast(BF16), "sil1")]
                      for j in range(2):
                          m_, p_ = tasks[n0 + j]
                          tasks[n0 + j] = (m_, (lambda p_=p_, j=j: (p_(), a1_hx_act(j, xs_pref[j][0], xs_pref[j][1], *hxb[j % 2]))))
                      for j in range(4):
                          pe_ = None
                          if j + 2 < 4:
                              pe_ = (lambda j=j: a1_hx_act(j + 2, xs_pref[j + 2][0], xs_pref[j + 2][1], *hxb[j % 2]))
                          a1p.append(a1_task(j, xs_pref[j][0], xs_pref[j][1], *hxb[j % 2], do_hx=False, post_extra=pe_))
                  add_proj_half(pa_sl, oagT, "oagT", False, half)
                  if half == 0:
                      if do_pref:
                          def pf_main():
                              dma(cosc[:], d_cos[:, 4 * (c + 1):4 * (c + 1) + 4, :], [], ["cosc"], "cos")
                              dma(sinc[:], d_sin[:, 4 * (c + 1):4 * (c + 1) + 4, :], [], ["sinc"], "sin")
                              for j in range(4):
                                  gt = 4 * (c + 1) + j
                                  dma(xs_pref[j][0], x[gt * 128:(gt + 1) * 128, :], [], [xs_pref[j][1]], f"xp{j}")
                          tasks.insert(len(tasks) - 2, (pf_main, lambda: None))
                      add_proj_half(pb_sl, obgT, "obgT", True, half)
                  else:
                      n1 = len(tasks)
                      add_proj_half(pb_sl, obgT, "obgT", True, half)
                      pbt = tasks[n1:]
                      del tasks[n1:]
                      mtt = []
                      for j in range(4):
                          st_ = {}
                          def main(j=j, st_=st_):
                              pT, tres = st_["t"] = nextT()
                              for kc in range(8):
                                  tr(pT[:, kc * 128:(kc + 1) * 128], mbf[:, j, kc * 128:(kc + 1) * 128], [f"mbf{j}"], [tres])
                          def post(j=j, st_=st_):
                              pT, tres = st_["t"]
                              A("act", lambda e: e.copy(mT[:, :, j * 128:(j + 1) * 128],
                                                        pT[:, :].rearrange("p (k n) -> p k n", k=8)), [tres], ["mT"])
                          mtt.append((main, post))
                      order_ = [pbt[0], pbt[1]] + a1p[0:1] + [pbt[2], mtt[0], pbt[3], mtt[1]] + a1p[1:2] + [mtt[2]] \
                          + a1p[2:3] + [mtt[3]] + a1p[3:4]
                      tasks.extend(order_)
              prefetched["hT"] = do_pref
              for half, sl in enumerate((SL_WO0, SL_WO1)):
                  for j in range(4):
                      st_ = {}
                      def main(j=j, st_=st_, sl=sl):
                          if j == 0:
                              wcur[sl] = nextw(sl)
                          wt, wres = wcur[sl]
                          wv = wt[:, :].rearrange("p (k n) -> p k n", k=8)
                          pb_, pres = st_["p"] = nextA()
                          for kc in range(8):
                              mm(pb_[:, :], mT[:, kc, j * 128:(j + 1) * 128], wv[:, kc, :], kc == 0, kc == 7, ["mT", wres], [pres])
                      def post(j=j, st_=st_, half=half):
                          pb_, pres = st_["p"]
                          A("dve", lambda e: e.tensor_add(
                              ybuf[:, j, half * 512:(half + 1) * 512], ybuf[:, j, half * 512:(half + 1) * 512], pb_[:, :]),
                            [pres, f"ybuf{j}"], [f"ybuf{j}"])
                          if half == 1:
                              final_norm(j)
                      tasks.append((main, post))

              pend_out = []

              def flush_out():
                  while pend_out:
                      jj = pend_out.pop(0)
                      gg = 4 * c + jj
                      dma(out[gg * 128:(gg + 1) * 128, :], ybuf[:, jj, :], [f"ybuf{jj}"], [], f"o{jj}", q="act")

              def final_norm(j):
                  gt = 4 * c + j
                  yj = f"ybuf{j}"
                  flush_out()
                  A("act", lambda e, j=j: e.activation(junk[:], ybuf[:, j, :], AF.Square, accum_out=ssf[:, j:j + 1]),
                    [yj], ["junk", f"ssf{j}"])
                  A("dve", lambda e, j=j: e.tensor_scalar(rsf[:, j:j + 1], ssf[:, j:j + 1], 1.0 / D, EPS, ALU.mult, ALU.add),
                    [f"ssf{j}"], [f"rsf{j}"])
                  A("pool", lambda e, j=j: e.tensor_tensor(rsf[:, j:j + 1], rsf[:, j:j + 1], neghalf[:], ALU.pow),
                    [f"rsf{j}", "neghalf"], [f"rsf{j}"])
                  A("dve", lambda e, j=j: e.scalar_tensor_tensor(ybuf[:, j, :], ybuf[:, j, :], rsf[:, j:j + 1], gfbc[:],
                                                                 ALU.mult, ALU.mult), [yj, f"rsf{j}", "gfbc"], [yj])
                  pend_out.append(j)

              pipeline(tasks)
              flush_out()
        except _Stop:
            pass
        sc.emit(final_sems=[k_ for k_ in [f"o{j}" for j in range(4)] + dbg_sems if k_ in sc.dcnt])
    return nc


_CACHE = {}


def _host_inputs(inputs):
    cst = _consts()
    f = lambda a: np.ascontiguousarray(np.asarray(a, dtype=np.float32))
    common = {
        "w_in": f(inputs["w_in"][0]),
        "w_pa": f(inputs["w_proj_a"][0]),
        "w_pb": f(inputs["w_proj_b"][0]),
        "w_out": f(inputs["w_out"][0]),
        "g_in": np.ascontiguousarray(f(inputs["norm_in_g"][0]).reshape(8, 128).T),
        "g_f": f(inputs["norm_f_g"]).reshape(1, D),
        "g_gn": f(inputs["gla_norm_g"][0]).reshape(1, 128),
        "b_mg": f(inputs["b_merge"][0]).reshape(1, 2048),
        "w_fg2": f(inputs["w_gla_fg2"][0]),
        "b_fg": f(inputs["b_gla_fg"][0]).reshape(1, 256),
        "c_ident": cst["ident"], "c_cb": cst["cb"], "c_cos2": cst["cos2"], "c_sin2": cst["sin2"],
        "c_triinc": cst["triinc"], "c_trirev": cst["trirev"], "c_mask01": cst["mask01"],
    }
    return common


def kernel(x, norm_in_g, w_in, b_merge, w_gla_fg2, b_gla_fg, gla_norm_g,
           w_proj_a, w_proj_b, w_out, norm_f_g):
    inputs = dict(x=x, norm_in_g=norm_in_g, w_in=w_in, b_merge=b_merge, w_gla_fg2=w_gla_fg2,
                  b_gla_fg=b_gla_fg, gla_norm_g=gla_norm_g, w_proj_a=w_proj_a, w_proj_b=w_proj_b,
                  w_out=w_out, norm_f_g=norm_f_g)
    common = _host_inputs(inputs)
    xs = np.asarray(x, dtype=np.float32)
    n = xs.shape[0]
    if "nc" not in _CACHE:
        _CACHE["nc"] = build()
    nc = _CACHE["nc"]
    in_maps = [dict(common, x=np.ascontiguousarray(xs[i])) for i in range(n)]
    res = run_bass_kernel_spmd(nc, in_maps, core_ids=list(range(n)))
    return np.stack([np.asarray(r["out"], dtype=np.float32) for r in res.results], axis=0)
```
